# Optimizing a Trainium2 kernel written in Bass

```python
import jax, jax.numpy as jnp
from jax import lax
import numpy as np

D_MODEL = 1024
BATCH = 32
SEQ = 2048
DEPTH = 2

CHUNK = 64
HEAD_DIM = 64
D_RWKV = D_MODEL // 2
N_HEADS = D_RWKV // HEAD_DIM
D_CONV = D_MODEL // 2
DECAY_LORA = 64
ICLR_LORA = 64
VRES_LORA = 32
GATE_LORA = 128
CONV_WIDTH = 3
D_FF = 2816
RMS_EPS = 1e-6
GN_EPS = HEAD_DIM * 1e-5

SHIFT_SIZES = (D_RWKV, D_RWKV, D_RWKV, DECAY_LORA, ICLR_LORA, GATE_LORA)
REST_SIZES = (D_CONV, D_CONV, D_CONV, D_MODEL, D_MODEL)
SHIFT_COLS = sum(SHIFT_SIZES)
C_IN = SHIFT_COLS + sum(REST_SIZES)

kernel_name = "hybrid_rwkv7_shortconv_convffn"


def _split(t, sizes):
    return jnp.split(t, np.cumsum(sizes)[:-1].tolist(), axis=-1)


def rms_norm(x, g):
    xf = x.astype(jnp.float32)
    y = xf * lax.rsqrt(jnp.mean(xf * xf, axis=-1, keepdims=True) + RMS_EPS)
    return (y * g.astype(jnp.float32)).astype(x.dtype)


def token_shift_lerp(p, mu):
    p_prev = jnp.pad(p, ((0, 0), (1, 0), (0, 0)))[:, :-1]
    return p + (p_prev - p) * mu


def causal_dwconv(x, w, b=None):
    c = x.shape[-1]
    y = lax.conv_general_dilated(x, w[:, None, :].astype(x.dtype), window_strides=(1,),
                                 padding=[(CONV_WIDTH - 1, 0)],
                                 dimension_numbers=('NWC', 'WIO', 'NWC'),
                                 feature_group_count=c)
    return y if b is None else y + b


def heads(t):
    return t.reshape(t.shape[0], t.shape[1], N_HEADS, HEAD_DIM)


def wkv7_scan(r, decay, k, v, kk, a):
    bsz, _, h, n = r.shape
    def step(S, inp):
        r_t, w_t, k_t, v_t, kk_t, a_t = inp
        sa = jnp.einsum('bhvk,bhk->bhv', S, kk_t)
        S = (S * w_t[:, :, None, :]
             - sa[..., None] * (kk_t * a_t)[:, :, None, :]
             + v_t[..., None] * k_t[:, :, None, :])
        return S, jnp.einsum('bhvk,bhk->bhv', S, r_t)
    xs = tuple(jnp.moveaxis(t.astype(jnp.float32), 1, 0) for t in (r, decay, k, v, kk, a))
    _, o = lax.scan(step, jnp.zeros((bsz, h, n, n), jnp.float32), xs)
    return jnp.moveaxis(o, 0, 1)


def setup_inputs(seed: int = 0) -> dict:
    key = jax.random.key(seed)
    ks = iter(jax.random.split(key, 32))
    L, Lv, D, F = DEPTH, DEPTH - 1, D_MODEL, D_FF
    nrm = lambda shape, s: jax.random.normal(next(ks), shape, jnp.float32) * s
    uni = lambda shape, lo, hi: jax.random.uniform(next(ks), shape, jnp.float32, lo, hi)
    return {
        "x": nrm((BATCH, SEQ, D), 1.0),
        "norm_mix_g": 1.0 + nrm((L, D), 0.02),
        "w_in": nrm((L, D, C_IN), D ** -0.5),
        "mu_shift": uni((L, SHIFT_COLS), 0.0, 1.0),
        "w0": uni((L, D_RWKV), -4.0, 0.0),
        "decay_up": nrm((L, DECAY_LORA, D_RWKV), 0.1),
        "a0": nrm((L, D_RWKV), 0.1),
        "a_up": nrm((L, ICLR_LORA, D_RWKV), 0.1),
        "g_up": nrm((L, GATE_LORA, D_RWKV), GATE_LORA ** -0.5),
        "k_k": 0.85 + nrm((L, D_RWKV), 0.05),
        "k_a": 1.0 + nrm((L, D_RWKV), 0.05),
        "r_k": nrm((L, N_HEADS, HEAD_DIM), 0.1),
        "ln_x_w": 1.0 + nrm((L, D_RWKV), 0.02),
        "ln_x_b": nrm((L, D_RWKV), 0.02),
        "vres_down": nrm((Lv, D, VRES_LORA), D ** -0.5),
        "vres_up": nrm((Lv, VRES_LORA, D_RWKV), 0.1),
        "v0": nrm((Lv, D_RWKV), 0.1),
        "conv_w": nrm((L, CONV_WIDTH, D_CONV), 0.5),
        "proj_a": nrm((L, D_RWKV, D), D_RWKV ** -0.5),
        "proj_b": nrm((L, D_CONV, D), D_CONV ** -0.5),
        "w_out": nrm((L, D, D), D ** -0.5),
        "norm_ffn_g": 1.0 + nrm((L, D), 0.02),
        "w_up": nrm((L, D, 2 * F), D ** -0.5),
        "ffn_conv_w": nrm((L, CONV_WIDTH, 2 * F), 0.5),
        "ffn_conv_b": nrm((L, 2 * F), 0.02),
        "w_down": nrm((L, F, D), F ** -0.5),
        "norm_final_g": 1.0 + nrm((D,), 0.02),
    }


def reference(x, norm_mix_g, w_in, mu_shift, w0, decay_up, a0, a_up, g_up, k_k, k_a, r_k,
              ln_x_w, ln_x_b, vres_down, vres_up, v0, conv_w, proj_a, proj_b, w_out,
              norm_ffn_g, w_up, ffn_conv_w, ffn_conv_b, w_down, norm_final_g):
    bsz, seq, _ = x.shape
    v_first = None
    for l in range(DEPTH):
        h = rms_norm(x, norm_mix_g[l])
        p = h @ w_in[l]
        shifted = token_shift_lerp(p[..., :SHIFT_COLS], mu_shift[l])
        r, k, v, wd, ad, gd = _split(shifted, SHIFT_SIZES)
        u, gate_b, gate_c, merge_a, merge_b = _split(p[..., SHIFT_COLS:], REST_SIZES)

        w_log = -jax.nn.softplus(-(w0[l] + jnp.tanh(wd) @ decay_up[l])) - 0.5
        decay = jnp.exp(-jnp.exp(w_log.astype(jnp.float32)))
        a = jax.nn.sigmoid(a0[l] + ad @ a_up[l])
        g = jax.nn.sigmoid(gd) @ g_up[l]
        if l == 0:
            v_first = v
        else:
            v = v + (v_first - v) * jax.nn.sigmoid(v0[l - 1] + (h @ vres_down[l - 1]) @ vres_up[l - 1])
        kk = heads(k * k_k[l]).astype(jnp.float32)
        kk = kk / jnp.maximum(jnp.linalg.norm(kk, axis=-1, keepdims=True), 1e-12)
        k = k * (1.0 + (a - 1.0) * k_a[l])
        rh, kh, vh = heads(r), heads(k), heads(v)
        o = wkv7_scan(rh, heads(decay), kh, vh, kk, heads(a))
        mu = jnp.mean(o, axis=-1, keepdims=True)
        var = jnp.mean(jnp.square(o - mu), axis=-1, keepdims=True)
        o = ((o - mu) * lax.rsqrt(var + GN_EPS)).reshape(bsz, seq, D_RWKV)
        o = (o * ln_x_w[l].astype(jnp.float32) + ln_x_b[l].astype(jnp.float32)).astype(x.dtype)
        bonus = jnp.sum(rh * kh * r_k[l], axis=-1, keepdims=True) * vh
        y_a = (o + bonus.reshape(bsz, seq, D_RWKV)) * g

        y_b = gate_b * causal_dwconv(gate_c * u, conv_w[l])

        merged = (jax.nn.sigmoid(merge_a) * (y_a @ proj_a[l])
                  + jax.nn.sigmoid(merge_b) * (y_b @ proj_b[l]))
        x = x + merged @ w_out[l]

        h = rms_norm(x, norm_ffn_g[l])
        z = causal_dwconv(h @ w_up[l], ffn_conv_w[l], ffn_conv_b[l])
        z_gate, z_up = jnp.split(z, 2, axis=-1)
        x = x + (jax.nn.silu(z_gate) * z_up) @ w_down[l]
    return rms_norm(x, norm_final_g)
```

```python
import math
import itertools
from contextlib import ExitStack

import numpy as np
import concourse.bass as bass
import concourse.mybir as mybir
from concourse.bass_utils import run_bass_kernel_spmd

F32 = mybir.dt.float32
BF16 = mybir.dt.bfloat16
AF = mybir.ActivationFunctionType
ALU = mybir.AluOpType
AX = mybir.AxisListType

N_CORES = 8
D = 1024
SEQ = 2048
TG = 512
NCH = TG // 128
DFF = 2816
NFC = DFF // 128
C0 = math.exp(-0.5)
RMS_EPS = 1e-6
GN_EPS = 64 * 1e-5
SLOT = 4096
NSLOT = 4
TPL = 39
SPL = 36
NROW = 5 * 512 + 1792
NCOL = 8 + 8 + 12 + 132 + 44
NLUP = 4 * 512 + 256 + 3 * 512
ARENA = 123008


class Sem:
    def __init__(self, h, owner):
        self.h, self.owner, self.count = h, owner, 0


class Buf:
    __slots__ = ("region", "lo", "hi", "writers", "readers", "name")

    def __init__(self, region, lo, hi, name=""):
        self.region, self.lo, self.hi, self.name = region, lo, hi, name
        self.writers, self.readers = {}, {}


class Trk:
    def __init__(self, nc, es):
        self.nc = nc
        self.eng = {"pe": nc.tensor, "act": nc.scalar, "dve": nc.vector, "pool": nc.gpsimd, "sp": nc.sync}
        self.esem = {k: Sem(es.enter_context(nc.semaphore("s_" + k)), k) for k in ("pe", "act", "dve", "pool")}
        self.es = es
        self.waited = {}
        self.regions = {}
        self.dsems = []
        self.ninst = 0

    def new_epoch(self):
        assert not getattr(self, "pending", False)
        self.epoch = getattr(self, "epoch", 0) + 1
        for k in ("pe", "act", "dve", "pool"):
            self.esem[k] = Sem(self.es.enter_context(self.nc.semaphore("s_%s_%d" % (k, self.epoch))), k)

    def dsem(self, name):
        s = Sem(self.es.enter_context(self.nc.semaphore(name)), None)
        self.dsems.append(s)
        return s

    def buf(self, region, lo=0, hi=1 << 30, name=""):
        b = Buf(region, lo, hi, name)
        self.regions.setdefault(region, []).append(b)
        return b

    def _deps(self, engname, reads, writes):
        deps = {}

        def add(sem, val, raw):
            if sem.owner == engname and engname == "pe":
                return
            if deps.get(sem, 0) < val:
                deps[sem] = val

        for b in reads:
            for o in self.regions[b.region]:
                if o.lo < b.hi and b.lo < o.hi:
                    for s, v in o.writers.items():
                        add(s, v, True)
        for b in writes:
            for o in self.regions[b.region]:
                if o.lo < b.hi and b.lo < o.hi:
                    for s, v in o.writers.items():
                        add(s, v, False)
                    for s, v in o.readers.items():
                        add(s, v, False)
        e = self.eng[engname]
        for s, v in deps.items():
            key = (engname, s)
            if self.waited.get(key, 0) < v:
                e.wait_ge(s.h, v)
                self.waited[key] = v

    def _record(self, sem, val, reads, writes):
        for b in reads:
            b.readers[sem] = val
        for b in writes:
            b.readers = {}
            b.writers = {sem: val}

    def emit(self, engname, fn, reads, writes, signal=True):
        self._deps(engname, reads, writes)
        inst = fn()
        s = self.esem[engname]
        if signal:
            s.count += 1
            inst.then_inc(s.h, 1)
            self._record(s, s.count, reads, writes)
            self.pending = False
        else:
            assert engname == "pe"
            self._record(s, s.count + 1, reads, writes)
            self.pending = True
        self.ninst += 1

    def dma(self, out, in_, sem, reads, writes):
        self._deps("sp", reads, writes)
        inst = self.nc.sync.dma_start(out=out, in_=in_)
        sem.count += 16
        inst.then_inc(sem.h, 16)
        self._record(sem, sem.count, reads, writes)
        self.ninst += 1


def build_program(n_seq=4, n_groups=SEQ // TG, n_layers=2):
    nc = bass.Bass("TRN2", target_bir_lowering=False)
    ntok = n_seq * n_groups * TG
    x_d = nc.dram_tensor("x", [ntok, D], F32, kind="ExternalInput").ap()
    wsrc_d = nc.dram_tensor("wsrc", [2 * SPL, 128, SLOT], F32, kind="ExternalInput").ap()
    rows_d = nc.dram_tensor("rows", [2, 128, NROW], F32, kind="ExternalInput").ap()
    cols_d = nc.dram_tensor("cols", [128, 2 * NCOL + 8], F32, kind="ExternalInput").ap()
    lup_d = nc.dram_tensor("lup", [2, 128, NLUP], F32, kind="ExternalInput").ap()
    cst_d = nc.dram_tensor("cst", [128, 4 * 128], F32, kind="ExternalInput").ap()
    brows_d = nc.dram_tensor("brows", [128, 1024], F32, kind="ExternalInput").ap()
    y_d = nc.dram_tensor("y", [ntok, D], F32, kind="ExternalOutput").ap()
    wbf_d = nc.dram_tensor("wbf", [2 * TPL, 128, SLOT], BF16, kind="Internal").ap()

    with ExitStack() as es:
        T = Trk(nc, es)

        def sb(name, shape, dt):
            return es.enter_context(nc.sbuf_tensor("sb_" + name, shape, dt))

        identf = sb("identf", [128, 128], F32)
        uiu_f = sb("uiu_f", [128, 128], F32)
        usu_f = sb("usu_f", [128, 128], F32)
        usl_f = sb("usl_f", [128, 128], F32)
        onescol = sb("onescol", [128, 2], F32)
        identb = sb("identb", [128, 128], BF16)
        onesb = sb("onesb", [128, 128], BF16)
        mask4 = sb("mask4", [128, 512], BF16)
        masksl = sb("masksl", [128, 128], BF16)
        rows = sb("rows", [128, 5, 512], F32)
        cols = sb("cols", [128, 2 * NCOL + 8], F32)
        dupx = sb("dupx", [128, 2, 512], BF16)
        aupx = sb("aupx", [128, 2, 512], BF16)
        gupb = sb("gupb", [128, 2, 512], BF16)
        vupx = sb("vupx", [128, 512], BF16)
        vdn = sb("vdn", [128, 8, 32], BF16)
        blo = sb("blo", [128, 2, 512], BF16)
        STf = sb("STf", [128, 2, 4, 128], F32)
        STb = sb("STb", [128, 2, 2, 4, 128], BF16)
        hprev = sb("hprev", [128, 2, 8], BF16)
        cutail = sb("cutail", [128, 2, 4, 2], F32)
        ztail = sb("ztail", [128, 2, 44, 2], F32)
        xT = sb("xT", [128, 8, TG], F32)
        ring = sb("ring", [128, NSLOT, SLOT], BF16)
        rstd = sb("rstd", [128, TG], F32)
        sqk = sb("sqk", [128, 2, TG], BF16)
        small = sb("small", [128, 64], F32)
        arena = sb("arena", [128, ARENA // 4], F32)
        ps = es.enter_context(nc.psum_tensor("ps", [128, 8, 512], F32))

        B = {}
        for nm in ("consts", "rows", "cols", "lora", "hprev", "cutail", "ztail", "rstd", "rsq", "small_wc", "small_wc1",
                   "small_a", "small_b", "small_c", "small_d", "small_e"):
            B[nm] = T.buf(nm)
        B_ST = [T.buf("STf%d" % l) for l in range(2)]
        B_STb = [[T.buf("STb%d_%d" % (l, p)) for p in range(2)] for l in range(2)]
        B_xT = [T.buf("xT%d" % k) for k in range(8)]
        B_sqk = [T.buf("sqk%d" % k) for k in range(2)]
        B_ring = [T.buf("ring%d" % k) for k in range(NSLOT)]
        B_ps = [T.buf("ps%d" % k) for k in range(8)]
        B_wbf = [T.buf("wbf%d" % k) for k in range(2 * TPL)]
        ring_sem = [T.dsem("rs%d" % k) for k in range(NSLOT)]
        ld_sem = T.dsem("ld")
        rows_sem = T.dsem("rowsld")
        st_sem = T.dsem("st")
        pin_sem = [T.dsem("pin%d" % k) for k in range(2)]
        pout_sem = [T.dsem("pout%d" % k) for k in range(4)]
        misc_sem = [T.dsem("misc%d" % k) for k in range(6)]

        def AB(off, nbytes, name=""):
            assert off % 4 == 0 and off + nbytes <= ARENA, (name, off, nbytes)
            return T.buf("arena", off, off + nbytes, name)

        def AV(off, dt, shape):
            n = int(np.prod(shape))
            if dt == F32:
                v = arena[:, off // 4: off // 4 + n]
            else:
                assert n % 2 == 0
                v = arena[:, off // 4: off // 4 + n // 2].bitcast(BF16)
            if len(shape) == 2:
                return v.rearrange("p (a b) -> p a b", a=shape[0])
            if len(shape) == 3:
                return v.rearrange("p (a b c) -> p a b c", a=shape[0], b=shape[1])
            if len(shape) == 4:
                return v.rearrange("p (a b c d) -> p a b c d", a=shape[0], b=shape[1], c=shape[2])
            return v

        o = 0
        hT = AV(o, BF16, [8, 514]); B_hT = AB(o, 8224, "hT"); o += 8224
        tw = AV(o, BF16, [512]); B_tw = AB(o, 1024); o += 1024
        adx = AV(o, BF16, [512]); B_adx = AB(o, 1024); o += 1024
        sgd = AV(o, BF16, [512]); B_sgd = AB(o, 1024); o += 1024
        hvx = AV(o, BF16, [512]); B_hvx = AB(o, 1024); o += 1024
        RAW_OFF = o
        raw = AV(o, F32, [3, NCH, 512]); B_raw = [[AB(o + (q * NCH + t) * 2048, 2048) for t in range(NCH)] for q in range(3)]; o += 3 * NCH * 2048
        yaT = AV(o, BF16, [4, TG]); B_yaT = AB(o, 4096); o += 4096
        ybT = AV(o, BF16, [4, TG]); B_ybT = [AB(o + j * 1024, 1024) for j in range(4)]; o += 4096
        TMP_OFF = o
        NTMP = 9
        tmp = [AV(o + i * 2048, F32, [512]) for i in range(NTMP)]
        B_tmp = [AB(o + i * 2048, 2048) for i in range(NTMP)]
        o += NTMP * 2048
        tmB = AV(o, BF16, [4, 512]); B_tmB = [AB(o + q * 1024, 1024) for q in range(4)]; o += 4096
        khat2 = [AV(o + i * 1024, BF16, [512]) for i in range(2)]; B_khat2 = [AB(o + i * 1024, 1024) for i in range(2)]; o += 2048
        bhat2 = [AV(o + i * 1024, BF16, [512]) for i in range(2)]; B_bhat2 = [AB(o + i * 1024, 1024) for i in range(2)]; o += 2048
        vbf2 = [AV(o + i * 1024, BF16, [512]) for i in range(2)]; B_vbf2 = [AB(o + i * 1024, 1024) for i in range(2)]; o += 2048
        tg2 = [AV(o + i * 2048, F32, [512]) for i in range(2)]; B_tg2 = [AB(o + i * 2048, 2048) for i in range(2)]; o += 4096
        tbv2 = [AV(o + i * 2048, F32, [512]) for i in range(2)]; B_tbv2 = [AB(o + i * 2048, 2048) for i in range(2)]; o += 4096
        FMPAD_OFF = o
        fmpad = AV(o, BF16, [4, 2, 2, 128]); B_fmpad = AB(o, 4096); o += 4096
        fmkb = AV(o, BF16, [4, 2, 128]); B_fmkb = AB(o, 2048); o += 2048
        AM = AV(o, BF16, [8, 512]); B_AM = [AB(o + h * 1024, 1024) for h in range(8)]; o += 8192
        PT = [AV(o + i * 2048, BF16, [8, 128]) for i in range(2)]
        B_PT = [[AB(o + i * 2048 + hf * 1024, 1024) for hf in range(2)] for i in range(2)]; o += 4096
        PY = [AV(o + i * 4096, BF16, [8, 2, 128]) for i in range(2)]
        B_PY = [[AB(o + i * 4096 + hf * 2048, 2048) for hf in range(2)] for i in range(2)]; o += 8192
        xtb = AV(o, BF16, [512]); B_xtb = AB(o, 1024); o += 1024
        satb = AV(o, BF16, [512]); B_satb = AB(o, 1024); o += 1024
        yatm = AV(o, BF16, [512]); B_yatm = AB(o, 1024); o += 1024
        VF_OFF = o
        vfirst = AV(o, F32, [NCH, 512]); B_vf = [AB(o + t * 2048, 2048) for t in range(NCH)]; o += NCH * 2048
        assert o <= ARENA, o
        o = RAW_OFF
        sga = [AV(o + i * 1024, BF16, [512]) for i in range(2)]; B_sga = [AB(o + i * 1024, 1024) for i in range(2)]; o += 2048
        sgb = [AV(o + i * 1024, BF16, [512]) for i in range(2)]; B_sgb = [AB(o + i * 1024, 1024) for i in range(2)]; o += 2048
        m1 = [AV(o + i * 2048, F32, [512]) for i in range(2)]; B_m1 = [AB(o + i * 2048, 2048) for i in range(2)]; o += 4096
        m2 = [AV(o + i * 2048, F32, [512]) for i in range(2)]; B_m2 = [AB(o + i * 2048, 2048) for i in range(2)]; o += 4096
        merged = AV(o, BF16, [8, TG]); B_merged = [AB(o + j * 1024, 1024) for j in range(8)]; o += 8192
        assert o <= RAW_OFF + 3 * NCH * 2048
        o = TMP_OFF
        usb = AV(o, F32, [512]); B_usb = AB(o, 2048); o += 2048
        cu = AV(o, F32, [514]); B_cu = AB(o, 2064); o += 2064
        cacc = AV(o, F32, [512]); B_cacc = AB(o, 2048); o += 2048
        o = 0
        h2T = AV(o, BF16, [8, TG]); B_h2T = AB(o, 8192); o += 8192
        actb = AV(o, BF16, [NFC, TG]); B_act = [AB(o + i * 1024, 1024) for i in range(NFC)]; o += NFC * 1024
        zg = [AV(o + i * 2064, F32, [514]) for i in range(2)]; B_zg = [AB(o + i * 2064, 2064) for i in range(2)]; o += 4128
        zu = [AV(o + i * 2064, F32, [514]) for i in range(2)]; B_zu = [AB(o + i * 2064, 2064) for i in range(2)]; o += 4128
        ag = [AV(o + i * 2048, F32, [512]) for i in range(2)]; B_ag = [AB(o + i * 2048, 2048) for i in range(2)]; o += 4096
        au = [AV(o + i * 2048, F32, [512]) for i in range(2)]; B_au = [AB(o + i * 2048, 2048) for i in range(2)]; o += 4096
        sgl = [AV(o + i * 2048, F32, [512]) for i in range(2)]; B_sgl = [AB(o + i * 2048, 2048) for i in range(2)]; o += 4096
        assert o <= VF_OFF
        xtm = AV(RAW_OFF, F32, [4, 1024]); B_xtm = AB(RAW_OFF, 16384, "xtm")
        yTf = AV(TMP_OFF, F32, [8, TG]); B_yT = [AB(TMP_OFF + k * 2048, 2048) for k in range(8)]
        pin = [AV(i * 16384, F32, [SLOT]) for i in range(2)]; B_pin = [AB(i * 16384, 16384) for i in range(2)]
        pout = [AV(32768 + i * 8192, BF16, [SLOT]) for i in range(4)]; B_pout = [AB(32768 + i * 8192, 8192) for i in range(4)]
        murow = AV(65536, F32, [1792]); B_mu = AB(65536, 7168)
        omrow = AV(65536 + 7168, F32, [1792]); B_om = AB(65536 + 7168, 7168)
        lupst = AV(81920, F32, [NLUP]); B_lupst = AB(81920, NLUP * 4)
        lotmp = AV(81920 + NLUP * 4, F32, [512]); B_lotmp = AB(81920 + NLUP * 4, 2048)
        cstst = AV(100352, F32, [512]); B_cstst = AB(100352, 2048)

        def mm(out, lhsT, rhs, start, stop, R, W, sig=None):
            T.emit("pe", lambda: nc.tensor.matmul(out, lhsT=lhsT, rhs=rhs, start=start, stop=stop), R, W,
                   signal=(stop if sig is None else sig))

        def tr(out, in_, ident, R, W):
            T.emit("pe", lambda: nc.tensor.transpose(out=out, in_=in_, identity=ident), R, W)

        def act(out, in_, func, R, W, scale=1.0, bias=0.0):
            T.emit("act", lambda: nc.scalar.activation(out=out, in_=in_, func=func, scale=scale, bias=bias), R, W)

        def cp(eng, out, in_, R, W):
            if eng == "act":
                T.emit("act", lambda: nc.scalar.copy(out=out, in_=in_), R, W)
            else:
                T.emit(eng, lambda: T.eng[eng].tensor_copy(out=out, in_=in_), R, W)

        def tt(eng, out, in0, in1, op, R, W):
            T.emit(eng, lambda: T.eng[eng].tensor_tensor(out=out, in0=in0, in1=in1, op=op), R, W)

        def ts(eng, out, in0, s1, s2, op0, op1, R, W):
            if s2 is None:
                T.emit(eng, lambda: T.eng[eng].tensor_scalar(out=out, in0=in0, scalar1=s1, scalar2=None, op0=op0), R, W)
            else:
                T.emit(eng, lambda: T.eng[eng].tensor_scalar(out=out, in0=in0, scalar1=s1, scalar2=s2, op0=op0, op1=op1), R, W)

        def stt(out, in0, scalar, in1, op0, op1, R, W):
            T.emit("dve", lambda: nc.vector.scalar_tensor_tensor(out=out, in0=in0, scalar=scalar, in1=in1, op0=op0, op1=op1), R, W)

        def red(out, in_, R, W):
            T.emit("dve", lambda: nc.vector.tensor_reduce(out=out, in_=in_, axis=AX.X, op=ALU.add), R, W)

        def recip(out, in_, R, W):
            T.emit("dve", lambda: nc.vector.reciprocal(out=out, in_=in_), R, W)

        def mset(eng, ap, val, W):
            T.emit(eng, lambda: T.eng[eng].memset(ap, val), [], W)

        pbank = [0]

        def nextbank():
            b = pbank[0]
            pbank[0] = (b + 1) % 6
            return b + 2

        pbank1 = [0]

        def nextbank_s1():
            pbank1[0] ^= 1
            return pbank1[0]

        rr = [0]

        def vp():
            rr[0] += 1
            return "pool" if rr[0] % 3 == 0 else "dve"

        ca = [0]

        def av():
            ca[0] += 1
            return "act" if ca[0] % 2 == 0 else "dve"

        T.dma(cstst, cst_d, misc_sem[0], [], [B_cstst])
        cp("dve", identf[:], cstst[:, 0:128], [B_cstst], [B["consts"]])
        cp("dve", uiu_f[:], cstst[:, 128:256], [B_cstst], [B["consts"]])
        cp("dve", usu_f[:], cstst[:, 256:384], [B_cstst], [B["consts"]])
        cp("dve", usl_f[:], cstst[:, 384:512], [B_cstst], [B["consts"]])
        cp("dve", identb[:], cstst[:, 0:128], [B_cstst], [B["consts"]])
        cp("dve", masksl[:], cstst[:, 384:512], [B_cstst], [B["consts"]])
        for q in range(4):
            src = cstst[:, 256:384] if q % 2 == 0 else cstst[:, 128:256]
            cp("dve", mask4[:, q * 128:(q + 1) * 128], src, [B_cstst], [B["consts"]])
        mset("pool", onesb[:], 1.0, [B["consts"]])
        mset("pool", onescol[:], 1.0, [B["consts"]])
        T.dma(cols[:], cols_d, misc_sem[1], [], [B["cols"]])
        T.dma(lupst[:, 0:1024], brows_d, misc_sem[4], [], [B_lupst])
        cp("dve", blo[:].rearrange("p a n -> p (a n)"), lupst[:, 0:1024], [B_lupst], [B["lora"]])
        cp("dve", lupst[:, 1024:2048], blo[:].rearrange("p a n -> p (a n)"), [B["lora"]], [B_lupst])
        tt("dve", lupst[:, 1024:2048], lupst[:, 0:1024], lupst[:, 1024:2048], ALU.subtract, [B_lupst], [B_lupst])
        cp("dve", blo[:].rearrange("p a n -> p (a n)"), lupst[:, 1024:2048], [B_lupst], [B["lora"]])
        for l in range(2):
            T.dma(lupst, lup_d[l], misc_sem[3], [B_lupst], [B_lupst])
            R_, W_ = [B_lupst], [B["lora"]]
            cp("dve", dupx[:, l, :], lupst[:, 0:512], R_, W_)
            cp("dve", aupx[:, l, :], lupst[:, 512:1024], R_, W_)
            cp("dve", gupb[:, l, :], lupst[:, 1024:1536], R_, W_)
            if l == 1:
                cp("dve", vupx[:], lupst[:, 1536:2048], R_, W_)
                cp("dve", vdn[:].rearrange("p a b -> p (a b)"), lupst[:, 2048:2304], R_, W_)
        def src_to_stream(l, s):
            if s == 0:
                return [(0, "lora")]
            if 1 <= s <= 3:
                return [(1 + 2 * (s - 1), "a"), (2 + 2 * (s - 1), "b")]
            if 4 <= s <= 6:
                return [(7 + s - 4, "c")]
            if 7 <= s <= 14:
                return [(10 + s - 7, "c")]
            if 15 <= s <= 16:
                return [(18 + s - 15, "c")]
            if 17 <= s <= 27:
                return [(20 + s - 17, "c")]
            return [(31 + s - 28, "c")]

        pi, po = 0, 0
        for l in range(n_layers):
            T.dma(murow, rows_d[l, :, 2560:2560 + 1792], misc_sem[5], [B_mu, B_om], [B_mu])
            ts("dve", omrow, murow, -1.0, 1.0, ALU.mult, ALU.add, [B_mu], [B_om])
            for s in range(SPL):
                bi_ = pi % 2
                pi += 1
                T.dma(pin[bi_], wsrc_d[l * SPL + s], pin_sem[bi_], [], [B_pin[bi_]])
                for (st, kind) in src_to_stream(l, s):
                    bo = po % 4
                    po += 1
                    eng = ("dve", "act", "pool")[po % 3]
                    if kind == "c":
                        cp(eng, pout[bo], pin[bi_], [B_pin[bi_]], [B_pout[bo]])
                    elif kind == "lora":
                        w3 = pin[bi_][:, 0:2048].rearrange("p (k n) -> p k n", k=8)
                        mu_b = murow[:, 1536:1792].unsqueeze(1).to_broadcast([128, 8, 256])
                        om_b = omrow[:, 1536:1792].unsqueeze(1).to_broadcast([128, 8, 256])
                        tt("dve", pout[bo][:, 0:2048].rearrange("p (k n) -> p k n", k=8), w3, om_b, ALU.mult,
                           [B_pin[bi_], B_om], [B_pout[bo]])
                        tt("dve", pout[bo][:, 2048:4096].rearrange("p (k n) -> p k n", k=8), w3, mu_b, ALU.mult,
                           [B_pin[bi_], B_mu], [B_pout[bo]])
                    else:
                        q = (st - 1) // 2
                        w3 = pin[bi_].rearrange("p (k n) -> p k n", k=8)
                        rowsrc = omrow if kind == "a" else murow
                        m_b = rowsrc[:, q * 512:(q + 1) * 512].unsqueeze(1).to_broadcast([128, 8, 512])
                        tt("dve" if kind == "a" else "pool", pout[bo].rearrange("p (k n) -> p k n", k=8), w3, m_b, ALU.mult,
                           [B_pin[bi_], B_mu, B_om], [B_pout[bo]])
                    T.dma(wbf_d[l * TPL + st], pout[bo], pout_sem[bo], [B_pout[bo]], [B_wbf[l * TPL + st]])

        for s_ in T.dsems:
            if s_.count:
                nc.sync.wait_ge(s_.h, s_.count)
                T.waited[("sp", s_)] = s_.count
        for b_ in B_wbf:
            b_.writers = {}
        mset("pool", arena[:, FMPAD_OFF // 4:FMPAD_OFF // 4 + 1024], 0.0, [B_fmpad])

        order = []
        for g_ in range(n_seq * n_groups):
            for l in range(n_layers):
                order += [l * TPL + k for k in range(TPL)]
        sstate = {"next_load": 0, "next_use": 0}

        def _load(i):
            slot = i % NSLOT
            T.dma(ring[:, slot, :], wbf_d[order[i]], ring_sem[slot], [B_wbf[order[i]]], [B_ring[slot]])

        def acquire(expect):
            i = sstate["next_use"]
            assert order[i] % TPL == expect, (order[i] % TPL, expect)
            if i == 0:
                for j in range(min(NSLOT - 1, len(order))):
                    _load(j)
                sstate["next_load"] = min(NSLOT - 1, len(order))
            nl = sstate["next_load"]
            if nl < len(order) and nl <= i + NSLOT - 1 and i >= 1:
                _load(nl)
                sstate["next_load"] = nl + 1
            sstate["next_use"] = i + 1
            slot = i % NSLOT
            return ring[:, slot, :], B_ring[slot]

        class ChunkStream:
            def __init__(self, first_tile):
                self.t, self.c, self.cur = first_tile, 0, None

            def next(self):
                if self.c % 4 == 0:
                    self.cur = acquire(self.t)
                    self.t += 1
                v = self.cur[0].rearrange("p (c k n) -> p c k n", c=4, k=8)[:, self.c % 4]
                self.c += 1
                return v, self.cur[1]

        def rmsnorm(gcol, dst_fn, B_dst_fn):
            bss = nextbank()
            for kc in range(8):
                k2 = kc % 2
                act(sqk[:, k2, :], xT[:, kc, :], AF.Square, [B_xT[kc]], [B_sqk[k2]])
                mm(ps[:, bss, :], onesb[:], sqk[:, k2, :], kc == 0, kc == 7, [B_sqk[k2], B["consts"]], [B_ps[bss]], sig=True)
            act(rstd[:], ps[:, bss, :], AF.Sqrt, [B_ps[bss]], [B["rstd"]], scale=1.0 / D, bias=RMS_EPS)
            recip(rstd[:], rstd[:], [B["rstd"]], [B["rstd"]])
            for kc in range(8):
                stt(dst_fn(kc), xT[:, kc, :], cols[:, gcol + kc:gcol + kc + 1], rstd[:], ALU.mult, ALU.mult,
                    [B_xT[kc], B["cols"], B["rstd"]], [B_dst_fn(kc)])

        def proj_fm(chunk, Bch, rhs_fn, R_rhs, nk=8):
            b = nextbank()
            for kc in range(nk):
                mm(ps[:, b, :], chunk[:, kc, :], rhs_fn(kc), kc == 0, kc == nk - 1, [Bch] + R_rhs, [B_ps[b]])
            return b

        def mixer(l, first_group):
            co = l * NCOL
            T.dma(rows[:], rows_d[l, :, 0:2560].rearrange("p (a b) -> p a b", a=5), rows_sem, [], [B["rows"]])
            cp("pool", hT[:, :, 1:2], hprev[:, l, :].unsqueeze(2), [B["hprev"]], [B_hT])
            rmsnorm(co, lambda kc: hT[:, kc, 2:514], lambda kc: B_hT)
            cp("pool", hprev[:, l, :].unsqueeze(2), hT[:, :, 513:514], [B_hT], [B["hprev"]])
            ha = lambda kc: hT[:, kc, 2:514]
            hb = lambda kc: hT[:, kc, 1:513]
            mset("pool", tw[64:65, :], 1.0, [B_tw])
            mset("pool", adx[64:65, :], 1.0, [B_adx])
            if l == 1:
                mset("pool", hvx[32:33, :], 1.0, [B_hvx])
            lt, Blt = acquire(0)
            la = lt[:, 0:2048].rearrange("p (k n) -> p k n", k=8)
            lb = lt[:, 2048:4096].rearrange("p (k n) -> p k n", k=8)
            for (c0, c1, dst, Bdst, fn) in ((0, 64, tw, B_tw, AF.Tanh), (64, 128, adx, B_adx, AF.Copy), (128, 256, sgd, B_sgd, AF.Sigmoid)):
                b = nextbank()
                m = c1 - c0
                for kc in range(8):
                    mm(ps[0:m, b, :], la[:, kc, c0:c1], ha(kc), kc == 0, False, [Blt, B_hT], [B_ps[b]])
                    mm(ps[0:m, b, :], lb[:, kc, c0:c1], hb(kc), False, kc == 7, [Blt, B_hT], [B_ps[b]])
                act(dst[0:m, :], ps[0:m, b, :], fn, [B_ps[b]], [Bdst])
            if l == 1:
                b = nextbank()
                for kc in range(8):
                    mm(ps[0:32, b, :], vdn[:, kc, :], ha(kc), kc == 0, kc == 7, [B["lora"], B_hT], [B_ps[b]])
                act(hvx[0:32, :], ps[0:32, b, :], AF.Copy, [B_ps[b]], [B_hvx])
            for q in range(3):
                wa, Bwa = acquire(1 + 2 * q)
                wb_, Bwb = acquire(2 + 2 * q)
                wa3 = wa.rearrange("p (k n) -> p k n", k=8)
                wb3 = wb_.rearrange("p (k n) -> p k n", k=8)
                for t in range(NCH):
                    b = nextbank()
                    for kc in range(8):
                        mm(ps[:, b, :], hT[:, kc, 2 + t * 128:2 + (t + 1) * 128], wa3[:, kc, :], kc == 0, False, [Bwa, B_hT], [B_ps[b]])
                        mm(ps[:, b, :], hT[:, kc, 1 + t * 128:1 + (t + 1) * 128], wb3[:, kc, :], False, kc == 7, [Bwb, B_hT], [B_ps[b]])
                    if q == 2 and l == 0:
                        cp(av(), vfirst[:, t, :], ps[:, b, :], [B_ps[b]], [B_vf[t]])
                    else:
                        cp(av(), raw[:, q, t, :], ps[:, b, :], [B_ps[b]], [B_raw[q][t]])
            cs = ChunkStream(7)
            cwo = co + 16
            for j in range(4):
                ch, Bch = cs.next()
                b = proj_fm(ch, Bch, ha, [B_hT])
                cp("act", usb, ps[:, b, :], [B_ps[b]], [B_usb])
                ch, Bch = cs.next()
                b = proj_fm(ch, Bch, ha, [B_hT])
                cp("pool", cu[:, 0:2], cutail[:, l, j, :], [B["cutail"]], [B_cu])
                tt("dve", cu[:, 2:514], ps[:, b, :], usb, ALU.mult, [B_ps[b], B_usb], [B_cu])
                cp("pool", cutail[:, l, j, :], cu[:, 512:514], [B_cu], [B["cutail"]])
                w = lambda k_: cols[:, cwo + j * 3 + k_:cwo + j * 3 + k_ + 1]
                ts("dve", cacc, cu[:, 0:512], w(0), None, ALU.mult, ALU.bypass, [B_cu, B["cols"]], [B_cacc])
                stt(cacc, cu[:, 1:513], w(1), cacc, ALU.mult, ALU.add, [B_cu, B_cacc, B["cols"]], [B_cacc])
                stt(cacc, cu[:, 2:514], w(2), cacc, ALU.mult, ALU.add, [B_cu, B_cacc, B["cols"]], [B_cacc])
                ch, Bch = cs.next()
                b = proj_fm(ch, Bch, ha, [B_hT])
                tt("dve", ybT[:, j, :], ps[:, b, :], cacc, ALU.mult, [B_ps[b], B_cacc], [B_ybT[j]])
            g0 = wkv_s1(l, 0)
            for _ in g0:
                pass
            pend = None
            for t in range(NCH):
                gens = [g_ for g_ in (pend, wkv_s1(l, t + 1) if t + 1 < NCH else None) if g_ is not None]
                filler = itertools.chain(*gens)
                wkv_s2(l, t, filler)
                for _ in filler:
                    pass
                wkv_s3a(l, t)
                pend = wkv_s3b(l, t)
            for _ in pend:
                pass
            for j in range(8):
                i2 = j % 2
                mp, Bmp = acquire(10 + j)
                mp3 = mp[:, 0:3072].rearrange("p (k n) -> p k n", k=24)
                b = nextbank()
                for kc in range(8):
                    mm(ps[:, b, :], mp3[:, kc, :], ha(kc), kc == 0, kc == 7, [Bmp, B_hT], [B_ps[b]])
                act(sga[i2], ps[:, b, :], AF.Sigmoid, [B_ps[b]], [B_sga[i2]])
                b = nextbank()
                for kc in range(4):
                    mm(ps[:, b, :], mp3[:, 16 + kc, :], yaT[:, kc, :], kc == 0, kc == 3, [Bmp, B_yaT], [B_ps[b]])
                tt("dve", m1[i2], ps[:, b, :], sga[i2], ALU.mult, [B_ps[b], B_sga[i2]], [B_m1[i2]])
                b = nextbank()
                for kc in range(8):
                    mm(ps[:, b, :], mp3[:, 8 + kc, :], ha(kc), kc == 0, kc == 7, [Bmp, B_hT], [B_ps[b]])
                act(sgb[i2], ps[:, b, :], AF.Sigmoid, [B_ps[b]], [B_sgb[i2]])
                b = nextbank()
                for kc in range(4):
                    mm(ps[:, b, :], mp3[:, 20 + kc, :], ybT[:, kc, :], kc == 0, kc == 3, [Bmp] + B_ybT, [B_ps[b]])
                tt("dve", m2[i2], ps[:, b, :], sgb[i2], ALU.mult, [B_ps[b], B_sgb[i2]], [B_m2[i2]])
                tt("pool", merged[:, j, :], m1[i2], m2[i2], ALU.add, [B_m1[i2], B_m2[i2]], [B_merged[j]])
            cs = ChunkStream(18)
            for j in range(8):
                ch, Bch = cs.next()
                b = proj_fm(ch, Bch, lambda kc: merged[:, kc, :], B_merged)
                tt("dve", xT[:, j, :], xT[:, j, :], ps[:, b, :], ALU.add, [B_xT[j], B_ps[b]], [B_xT[j]])

        def wkv_s1(l, t):
            tok = slice(t * 128, (t + 1) * 128)
            db = t % 2
            khat, bhat, vbf = khat2[db], bhat2[db], vbf2[db]
            B_khat, B_bhat, B_vbf = B_khat2[db], B_bhat2[db], B_vbf2[db]
            r_t, k_t = raw[:, 0, t, :], raw[:, 1, t, :]
            Br, Bk = B_raw[0][t], B_raw[1][t]
            row = lambda i: rows[:, i, :]
            (t_sw, t_a, t_v, t_kap, t_kp, t_nb, t_e0, t_e1, t_x) = tmp
            (Bsw, Ba, Bv, Bkap, Bkp, Bnb, Be0, Be1, Bx) = B_tmp
            t_g, Bg, t_bv, Bbv = tg2[db], B_tg2[db], tbv2[db], B_tbv2[db]
            wcn = "small_wc" if db == 0 else "small_wc1"
            LO, CO_ = B["lora"], B["consts"]
            b = nextbank_s1()
            mm(ps[:, b, :], tw[0:65, tok], dupx[0:65, l, :], True, False, [B_tw, LO], [B_ps[b]])
            r_ = l * 2
            mm(ps[:, b, :], onesb[32 * (r_ % 3):32 * (r_ % 3) + 1, :], blo[32 * (r_ % 3):32 * (r_ % 3) + 1, r_ // 3, :], False, True, [CO_, LO], [B_ps[b]])
            act(t_sw, ps[:, b, :], AF.Sigmoid, [B_ps[b]], [Bsw])
            yield
            b = nextbank_s1()
            mm(ps[:, b, :], adx[0:65, tok], aupx[0:65, l, :], True, False, [B_adx, LO], [B_ps[b]])
            r_ = l * 2 + 1
            mm(ps[:, b, :], onesb[32 * (r_ % 3):32 * (r_ % 3) + 1, :], blo[32 * (r_ % 3):32 * (r_ % 3) + 1, r_ // 3, :], False, True, [CO_, LO], [B_ps[b]])
            act(t_a, ps[:, b, :], AF.Sigmoid, [B_ps[b]], [Ba])
            yield
            b = nextbank_s1()
            mm(ps[:, b, :], sgd[:, tok], gupb[:, l, :], True, True, [B_sgd, LO], [B_ps[b]])
            cp("act", t_g, ps[:, b, :], [B_ps[b]], [Bg])
            yield
            if l == 1:
                b = nextbank_s1()
                mm(ps[:, b, :], hvx[0:33, tok], vupx[0:33, :], True, False, [B_hvx, LO], [B_ps[b]])
                r_ = 4
                mm(ps[:, b, :], onesb[32 * (r_ % 3):32 * (r_ % 3) + 1, :], blo[32 * (r_ % 3):32 * (r_ % 3) + 1, r_ // 3, :], False, True, [CO_, LO], [B_ps[b]])
                act(t_x, ps[:, b, :], AF.Sigmoid, [B_ps[b]], [Bx])
                v_raw = raw[:, 2, t, :]
                tt("dve", t_v, vfirst[:, t, :], v_raw, ALU.subtract, [B_vf[t], B_raw[2][t]], [Bv])
                tt("pool", t_v, t_v, t_x, ALU.mult, [Bv, Bx], [Bv])
                tt("pool", t_v, t_v, v_raw, ALU.add, [Bv, B_raw[2][t]], [Bv])
                v_eff, Bve = t_v, Bv
            else:
                v_eff, Bve = vfirst[:, t, :], B_vf[t]
            WC = small[:, 48 + db * 8:48 + db * 8 + 8]
            tt("dve", t_kap, k_t, row(0), ALU.mult, [Bk, B["rows"]], [Bkap])
            yield
            tt("pool", t_x, t_kap, t_kap, ALU.mult, [Bkap], [Bx])
            yield
            ss = small[:, 8:16]
            red(ss, t_x.rearrange("p (h n) -> p h n", h=8), [Bx], [B["small_a"]])
            yield
            act(ss, ss, AF.Sqrt, [B["small_a"]], [B["small_a"]])
            yield
            ts("dve", ss, ss, 1e-12, None, ALU.max, ALU.bypass, [B["small_a"]], [B["small_a"]])
            yield
            recip(ss, ss, [B["small_a"]], [B["small_a"]])
            yield
            k3 = t_kap.rearrange("p (h n) -> p h n", h=8)
            tt("dve", k3, k3, ss.unsqueeze(2).to_broadcast([128, 8, 64]), ALU.mult, [Bkap, B["small_a"]], [Bkap])
            yield
            stt(t_kp, t_a, -1.0, row(1), ALU.add, ALU.mult, [Ba, B["rows"]], [Bkp])
            yield
            stt(t_kp, t_kp, 1.0, k_t, ALU.add, ALU.mult, [Bkp, Bk], [Bkp])
            yield
            stt(t_nb, t_a, -1.0, t_kap, ALU.mult, ALU.mult, [Ba, Bkap], [Bnb])
            yield
            bce = nextbank_s1()
            mm(ps[:, bce, :], usu_f[:], t_sw, True, True, [CO_, Bsw], [B_ps[bce]])
            act(t_e0, ps[:, bce, :], AF.Exp, [B_ps[bce]], [Be0], scale=-C0)
            yield
            tt("dve", tmB[:, 0, :], t_kap, t_e0, ALU.mult, [Bkap, Be0], [B_tmB[0]])
            bci = nextbank_s1()
            mm(ps[:, bci, :], uiu_f[:], t_sw, True, True, [CO_, Bsw], [B_ps[bci]])
            act(t_e1, ps[:, bci, :], AF.Exp, [B_ps[bci]], [Be1], scale=-C0)
            yield
            tt("pool", tmB[:, 1, :], r_t, t_e1, ALU.mult, [Br, Be1], [B_tmB[1]])
            act(t_e0, ps[:, bci, :], AF.Exp, [B_ps[bci]], [Be0], scale=C0)
            yield
            tt("dve", tmB[:, 2, :], t_kp, t_e0, ALU.mult, [Bkp, Be0], [B_tmB[2]])
            tt("pool", tmB[:, 3, :], t_nb, t_e0, ALU.mult, [Bnb, Be0], [B_tmB[3]])
            yield
            brv = nextbank_s1()
            mm(ps[:, brv, :], usl_f[:], t_sw, True, True, [CO_, Bsw], [B_ps[brv]])
            act(t_e1, ps[:, brv, :], AF.Exp, [B_ps[brv]], [Be1], scale=-C0)
            yield
            tt("dve", khat, t_kp, t_e1, ALU.mult, [Bkp, Be1], [B_khat])
            yield
            tt("pool", bhat, t_nb, t_e1, ALU.mult, [Bnb, Be1], [B_bhat])
            yield
            yield
            btot = nextbank_s1()
            for p in range(4):
                mm(ps[:, btot, 2 * p:2 * p + 2], t_sw[:, p * 128:(p + 1) * 128], onescol[:, 0:2], True, True, [CO_, Bsw], [B_ps[btot]])
            act(WC, ps[:, btot, 0:8], AF.Exp, [B_ps[btot]], [B[wcn]], scale=-C0)
            yield
            yield
            cp("pool", vbf, v_eff, [Bve], [B_vbf])
            yield
            tt("dve", t_x, r_t, t_kp, ALU.mult, [Br, Bkp], [Bx])
            yield
            tt("pool", t_x, t_x, row(2), ALU.mult, [Bx, B["rows"]], [Bx])
            yield
            bs = small[:, 16:24]
            red(bs, t_x.rearrange("p (h n) -> p h n", h=8), [Bx], [B["small_b"]])
            yield
            tt("dve", t_bv.rearrange("p (h n) -> p h n", h=8), v_eff.rearrange("p (h n) -> p h n", h=8),
               bs.unsqueeze(2).to_broadcast([128, 8, 64]), ALU.mult, [Bve, B["small_b"]], [Bbv])

        def wkv_s2(l, t, filler):
            CO_ = B["consts"]

            def fill(n):
                if filler is None:
                    return
                for _ in range(n):
                    try:
                        next(filler)
                    except StopIteration:
                        return
            for half in range(2):
                b = nextbank()
                psb = ps[:, b, :].bitcast(BF16)
                for pp in range(2):
                    p = half * 2 + pp
                    for q in range(4):
                        tr(psb[:, (pp * 4 + q) * 128:(pp * 4 + q + 1) * 128], tmB[:, q, p * 128:(p + 1) * 128], identb[:],
                           [B_tmB[q], CO_], [B_ps[b]])
                v4 = psb.rearrange("p (a q n) -> p a q n", a=2, q=4)
                for hh in range(2):
                    cp("act" if hh == 0 else "dve", fmpad[hh * 64:(hh + 1) * 64, half * 2:half * 2 + 2, hh, :, :],
                       v4[hh * 64:(hh + 1) * 64, :, 0:2, :], [B_ps[b]], [B_fmpad])
                cp("act", fmkb[:, half * 2:half * 2 + 2, :, :], v4[:, :, 2:4, :], [B_ps[b]], [B_fmkb])
                fill(2)
            for h in range(8):
                p, hh = h // 2, h % 2
                b = nextbank()
                rhs = fmpad[:, p, hh, :, :].rearrange("p a n -> p (a n)")
                mm(ps[:, b, 0:256], fmkb[:, p, 0, :], rhs, True, True, [B_fmkb, B_fmpad], [B_ps[b]])
                mm(ps[:, b, 256:512], fmkb[:, p, 1, :], rhs, True, True, [B_fmkb, B_fmpad], [B_ps[b]])
                tt("dve", AM[:, h, :], ps[:, b, :], mask4[:], ALU.mult, [B_ps[b], CO_], [B_AM[h]])
                if h % 2 == 1:
                    fill(1)
            for half in range(2):
                b = nextbank()
                for j in range(4):
                    h = half * 4 + j
                    p, hh = h // 2, h % 2
                    mm(ps[:, b, j * 128:(j + 1) * 128], fmpad[:, p, hh, 0, :], fmkb[:, p, 1, :], True, True, [B_fmkb, B_fmpad], [B_ps[b]])
                tt("dve", PT[0][:, half * 4:half * 4 + 4, :], ps[:, b, :].rearrange("p (a n) -> p a n", a=4),
                   masksl[:].unsqueeze(1).to_broadcast([128, 4, 128]), ALU.mult, [B_ps[b], CO_], [B_PT[0][half]])
                hs = slice(half * 4, half * 4 + 4)
                cp("pool", PY[0][:, hs, 0, :], AM[:, hs, 256:384], B_AM[half * 4:half * 4 + 4], [B_PY[0][half]])
                tt("pool", PY[1][:, hs, 1, :], AM[:, hs, 256:384], identb[:].unsqueeze(1).to_broadcast([128, 4, 128]), ALU.add,
                   B_AM[half * 4:half * 4 + 4] + [CO_], [B_PY[1][half]])
            for kk in range(1, 8):
                cur, nxt = (kk - 1) % 2, kk % 2
                for half in range(2):
                    Bc = [B_PT[cur][half], B_PY[cur][half]]
                    if kk == 1:
                        Bc = Bc
                    bA, bB = nextbank(), nextbank()
                    bC = nextbank() if kk <= 6 else None
                    for j in range(4):
                        h = half * 4 + j
                        bank = bA if j < 2 else bB
                        off = (j % 2) * 256
                        if kk == 1:
                            mm(ps[:, bank, off:off + 128], PT[cur][:, h, :], PY[cur][:, h, 0, :], True, True, Bc, [B_ps[bank]])
                        elif kk <= 6:
                            mm(ps[:, bank, off:off + 256], PT[cur][:, h, :], PY[cur][:, h, :, :].rearrange("p a n -> p (a n)"), True, True, Bc, [B_ps[bank]])
                        else:
                            mm(ps[:, bank, off + 128:off + 256], PT[cur][:, h, :], PY[cur][:, h, 1, :], True, True, Bc, [B_ps[bank]])
                        if kk <= 6:
                            mm(ps[:, bC, j * 128:(j + 1) * 128], PY[cur][:, h, 0, :], PT[cur][:, h, :], True, True, Bc, [B_ps[bC]])
                    for jj, bank in enumerate((bA, bB)):
                        hs = slice(half * 4 + jj * 2, half * 4 + jj * 2 + 2)
                        v3 = ps[:, bank, :].rearrange("p (a q n) -> p a q n", a=2, q=2)
                        if kk <= 6:
                            cp("act", PY[nxt][:, hs, 0, :], v3[:, :, 0, :], [B_ps[bank]], [B_PY[nxt][half]])
                        if kk >= 2:
                            tt("dve", PY[nxt][:, hs, 1, :], v3[:, :, 1, :], PY[cur][:, hs, 1, :], ALU.add, [B_ps[bank], B_PY[cur][half]], [B_PY[nxt][half]])
                    if kk <= 6:
                        cp("act", PT[nxt][:, half * 4:half * 4 + 4, :], ps[:, bC, :].rearrange("p (a n) -> p a n", a=4), [B_ps[bC]], [B_PT[nxt][half]])
                    fill(4)
        def wkv_s3a(l, t):
            par = t % 2
            db = t % 2
            khat, bhat, vbf = khat2[db], bhat2[db], vbf2[db]
            B_khat, B_bhat, B_vbf = B_khat2[db], B_bhat2[db], B_vbf2[db]
            (t_sw, t_a, t_v, t_kap, t_kp, t_nb, t_e0, t_e1, t_x) = tmp
            (Bsw, Ba, Bv, Bkap, Bkp, Bnb, Be0, Be1, Bx) = B_tmp
            t_g, Bg, t_bv, Bbv = tg2[db], B_tg2[db], tbv2[db], B_tbv2[db]
            wcn = "small_wc" if db == 0 else "small_wc1"
            WC = small[:, 48 + db * 8:48 + db * 8 + 8]
            row = lambda i: rows[:, i, :]
            CO_ = B["consts"]
            fin = 7 % 2
            Tm = lambda h: PY[fin][:, h, 1, :]
            BT = B_PY[fin]
            stb_old, Bst_old = STb[:, l, par], B_STb[l][par]
            bX = nextbank()
            for h in range(8):
                p, hh = h // 2, h % 2
                hc = slice(h * 64, (h + 1) * 64)
                mm(ps[:, bX, hc], fmpad[:, p, hh, 0, :], stb_old[:, p, hh * 64:(hh + 1) * 64], True, False, [B_fmpad, Bst_old], [B_ps[bX]])
                mm(ps[:, bX, hc], AM[:, h, 0:128], vbf[:, hc], False, True, [B_AM[h], B_vbf], [B_ps[bX]])
            cp("act", xtb, ps[:, bX, :], [B_ps[bX]], [B_xtb])
            bS = nextbank()
            for h in range(8):
                hc = slice(h * 64, (h + 1) * 64)
                mm(ps[:, bS, hc], Tm(h), xtb[:, hc], True, True, [BT[h // 4], B_xtb], [B_ps[bS]])
            cp("dve", satb, ps[:, bS, :], [B_ps[bS]], [B_satb])
            bO = nextbank()
            for h in range(8):
                p, hh = h // 2, h % 2
                hc = slice(h * 64, (h + 1) * 64)
                mm(ps[:, bO, hc], fmpad[:, p, hh, 1, :], stb_old[:, p, hh * 64:(hh + 1) * 64], True, False, [B_fmpad, Bst_old], [B_ps[bO]])
                mm(ps[:, bO, hc], AM[:, h, 128:256], vbf[:, hc], False, False, [B_AM[h], B_vbf], [B_ps[bO]])
                mm(ps[:, bO, hc], AM[:, h, 384:512], satb[:, hc], False, True, [B_AM[h], B_satb], [B_ps[bO]])
            bU = nextbank()
            for p in range(4):
                pc = slice(p * 128, (p + 1) * 128)
                mm(ps[:, bU, pc], khat[:, pc], vbf[:, pc], True, False, [B_khat, B_vbf], [B_ps[bU]])
                mm(ps[:, bU, pc], bhat[:, pc], satb[:, pc], False, True, [B_bhat, B_satb], [B_ps[bU]])
            for p in range(4):
                stt(STf[:, l, p, :], STf[:, l, p, :], WC[:, 2 * p:2 * p + 1], ps[:, bU, p * 128:(p + 1) * 128], ALU.mult, ALU.add,
                    [B_ST[l], B[wcn], B_ps[bU]], [B_ST[l]])
            cp("act", STb[:, l, 1 - par].rearrange("p a n -> p (a n)"), STf[:, l].rearrange("p a n -> p (a n)"), [B_ST[l]], [B_STb[l][1 - par]])
            cp("act", t_kap, ps[:, bO, :], [B_ps[bO]], [Bkap])

        def wkv_s3b(l, t):
            par = t % 2
            db = t % 2
            khat, bhat, vbf = khat2[db], bhat2[db], vbf2[db]
            B_khat, B_bhat, B_vbf = B_khat2[db], B_bhat2[db], B_vbf2[db]
            (t_sw, t_a, t_v, t_kap, t_kp, t_nb, t_e0, t_e1, t_x) = tmp
            (Bsw, Ba, Bv, Bkap, Bkp, Bnb, Be0, Be1, Bx) = B_tmp
            t_g, Bg, t_bv, Bbv = tg2[db], B_tg2[db], tbv2[db], B_tbv2[db]
            wcn = "small_wc" if db == 0 else "small_wc1"
            WC = small[:, 48 + db * 8:48 + db * 8 + 8]
            row = lambda i: rows[:, i, :]
            CO_ = B["consts"]
            o_sb, Bo = t_kap, Bkap
            cen, Bcen = t_kp, Bkp
            sq_, Bsq = t_nb, Bnb
            mean = small[:, 24:32]
            red(mean, o_sb.rearrange("p (h n) -> p h n", h=8), [Bo], [B["small_c"]])
            yield
            ts("dve", mean, mean, 1.0 / 64, None, ALU.mult, ALU.bypass, [B["small_c"]], [B["small_c"]])
            yield
            o3, c3 = o_sb.rearrange("p (h n) -> p h n", h=8), cen.rearrange("p (h n) -> p h n", h=8)
            tt("dve", c3, o3, mean.unsqueeze(2).to_broadcast([128, 8, 64]), ALU.subtract, [Bo, B["small_c"]], [Bcen])
            yield
            tt("pool", sq_, cen, cen, ALU.mult, [Bcen], [Bsq])
            yield
            var = small[:, 32:40]
            red(var, sq_.rearrange("p (h n) -> p h n", h=8), [Bsq], [B["small_d"]])
            yield
            act(var, var, AF.Sqrt, [B["small_d"]], [B["small_d"]], scale=1.0 / 64, bias=GN_EPS)
            yield
            recip(var, var, [B["small_d"]], [B["small_d"]])
            yield
            tt("dve", c3, c3, var.unsqueeze(2).to_broadcast([128, 8, 64]), ALU.mult, [Bcen, B["small_d"]], [Bcen])
            yield
            tt("pool", cen, cen, row(3), ALU.mult, [Bcen, B["rows"]], [Bcen])
            yield
            tt("dve", cen, cen, row(4), ALU.add, [Bcen, B["rows"]], [Bcen])
            yield
            tt("pool", cen, cen, t_bv, ALU.add, [Bcen, Bbv], [Bcen])
            yield
            tt("dve", yatm, cen, t_g, ALU.mult, [Bcen, Bg], [B_yatm])
            yield
            b = nextbank()
            psb = ps[:, b, :].bitcast(BF16)
            for p in range(4):
                tr(psb[:, p * 128:(p + 1) * 128], yatm[:, p * 128:(p + 1) * 128], identb[:], [B_yatm, CO_], [B_ps[b]])
            cp("act", yaT[:, :, t * 128:(t + 1) * 128], psb[:, 0:512].rearrange("p (a n) -> p a n", a=4), [B_ps[b]], [B_yaT])
            yield

        def ffn(l):
            co = l * NCOL
            rmsnorm(co + 8, lambda kc: h2T[:, kc, :], lambda kc: B_h2T)
            fwo, fbo = co + 28, co + 28 + 132
            hr = lambda kc: h2T[:, kc, :]
            cur = None

            def ffn_fin(i_):
                j2 = i_ % 2
                act(sgl[j2], ag[j2], AF.Silu, [B_ag[j2]], [B_sgl[j2]])
                tt("pool", actb[:, i_, :], sgl[j2], au[j2], ALU.mult, [B_sgl[j2], B_au[j2]], [B_act[i_]])

            for i in range(NFC):
                i2 = i % 2
                if i % 2 == 0:
                    cur = acquire(20 + i // 2)
                c4 = cur[0].rearrange("p (c k n) -> p c k n", c=4, k=8)
                bg = proj_fm(c4[:, i2], cur[1], hr, [B_h2T])
                bu = proj_fm(c4[:, 2 + i2], cur[1], hr, [B_h2T])
                for (bank, z, Bz, a_, Ba_, fc) in ((bg, zg[i2], B_zg[i2], ag[i2], B_ag[i2], i), (bu, zu[i2], B_zu[i2], au[i2], B_au[i2], NFC + i)):
                    cp("pool", z[:, 0:2], ztail[:, l, fc, :], [B["ztail"]], [Bz])
                    cp("act", z[:, 2:514], ps[:, bank, :], [B_ps[bank]], [Bz])
                    cp("pool", ztail[:, l, fc, :], z[:, 512:514], [Bz], [B["ztail"]])
                    w = lambda k_: cols[:, fwo + fc * 3 + k_:fwo + fc * 3 + k_ + 1]
                    if fc < NFC:
                        act(a_, z[:, 2:514], AF.Identity, [Bz, B["cols"]], [Ba_], scale=w(2), bias=cols[:, fbo + fc:fbo + fc + 1])
                    else:
                        ts("dve", a_, z[:, 2:514], w(2), cols[:, fbo + fc:fbo + fc + 1], ALU.mult, ALU.add, [Bz, B["cols"]], [Ba_])
                    stt(a_, z[:, 1:513], w(1), a_, ALU.mult, ALU.add, [Bz, Ba_, B["cols"]], [Ba_])
                    stt(a_, z[:, 0:512], w(0), a_, ALU.mult, ALU.add, [Bz, Ba_, B["cols"]], [Ba_])
                if i >= 1:
                    ffn_fin(i - 1)
            ffn_fin(NFC - 1)
            for j in range(8):
                wd_, Bwd = acquire(31 + j)
                wd3 = wd_[:, 0:NFC * 128].rearrange("p (k n) -> p k n", k=NFC)
                b = nextbank()
                for fc in range(NFC):
                    mm(ps[:, b, :], wd3[:, fc, :], actb[:, fc, :], fc == 0, fc == NFC - 1, [Bwd, B_act[fc]], [B_ps[b]])
                tt("dve", xT[:, j, :], xT[:, j, :], ps[:, b, :], ALU.add, [B_xT[j], B_ps[b]], [B_xT[j]])

        for s in range(n_seq):
            for g in range(n_groups):
                r0 = (s * n_groups + g) * TG
                if g == 0:
                    T.new_epoch()
                if g == 0:
                    mset("pool", STf[:], 0.0, B_ST)
                    mset("pool", STb[:], 0.0, [B_STb[0][0], B_STb[0][1], B_STb[1][0], B_STb[1][1]])
                    mset("pool", hprev[:], 0.0, [B["hprev"]])
                    mset("pool", cutail[:], 0.0, [B["cutail"]])
                    mset("pool", ztail[:], 0.0, [B["ztail"]])
                T.dma(xtm, x_d[r0:r0 + TG, :].rearrange("(t p) d -> p t d", p=128), ld_sem, [], [B_xtm])
                for kc in range(8):
                    b = nextbank()
                    for t in range(4):
                        tr(ps[:, b, t * 128:(t + 1) * 128], xtm[:, t, kc * 128:(kc + 1) * 128], identf[:], [B_xtm, B["consts"]], [B_ps[b]])
                    cp(av(), xT[:, kc, :], ps[:, b, :], [B_ps[b]], [B_xT[kc]])
                for l in range(n_layers):
                    mixer(l, g == 0)
                    ffn(l)
                rmsnorm(2 * NCOL, lambda kc: yTf[:, kc, :], lambda kc: B_yT[kc])
                for t in range(4):
                    for hf in range(2):
                        b = nextbank()
                        for kq in range(4):
                            kc = hf * 4 + kq
                            tr(ps[:, b, kq * 128:(kq + 1) * 128], yTf[:, kc, t * 128:(t + 1) * 128], identf[:], [B_yT[kc], B["consts"]], [B_ps[b]])
                        cp(av(), xtm[:, t, hf * 512:(hf + 1) * 512], ps[:, b, :], [B_ps[b]], [B_xtm])
                T.dma(y_d[r0:r0 + TG, :].rearrange("(t p) d -> p t d", p=128), xtm, st_sem, [B_xtm], [])
        for s_ in T.dsems:
            if s_.count:
                nc.sync.wait_ge(s_.h, s_.count)
        assert not T.pending
        print("[kernel] instructions emitted:", T.ninst, {k: v.count for k, v in T.esem.items()})
    return nc


def _km(W):
    return np.ascontiguousarray(W.reshape(-1, 128, W.shape[1]).transpose(1, 0, 2))


def _pad_tile(a):
    a = a.reshape(128, -1)
    out = np.zeros((128, SLOT), np.float32)
    out[:, :a.shape[1]] = a
    return out


def host_layout(inp):
    f = lambda k: np.asarray(inp[k], np.float32)
    w_in, proj_a, proj_b, w_out, w_up, w_down = f("w_in"), f("proj_a"), f("proj_b"), f("w_out"), f("w_up"), f("w_down")
    wsrc = np.zeros((2 * SPL, 128, SLOT), np.float32)
    for l in range(2):
        t = []
        t.append(_pad_tile(_km(w_in[l][:, 1536:1792])))
        for q in range(3):
            t.append(_pad_tile(_km(w_in[l][:, q * 512:(q + 1) * 512])))
        chunks = []
        for j in range(4):
            chunks += [1792 + j * 128, 2816 + j * 128, 2304 + j * 128]
        for i in range(3):
            t.append(_pad_tile(np.stack([_km(w_in[l][:, c:c + 128]) for c in chunks[i * 4:(i + 1) * 4]], axis=1)))
        for j in range(8):
            t.append(_pad_tile(np.concatenate([_km(w_in[l][:, 3328 + j * 128:3328 + (j + 1) * 128]),
                                               _km(w_in[l][:, 4352 + j * 128:4352 + (j + 1) * 128]),
                                               _km(proj_a[l][:, j * 128:(j + 1) * 128]),
                                               _km(proj_b[l][:, j * 128:(j + 1) * 128])], axis=1)))
        for i in range(2):
            t.append(_pad_tile(np.stack([_km(w_out[l][:, c * 128:(c + 1) * 128]) for c in range(i * 4, i * 4 + 4)], axis=1)))
        for i in range(11):
            cc = [2 * i * 128, (2 * i + 1) * 128, DFF + 2 * i * 128, DFF + (2 * i + 1) * 128]
            t.append(_pad_tile(np.stack([_km(w_up[l][:, c:c + 128]) for c in cc], axis=1)))
        for j in range(8):
            t.append(_pad_tile(_km(w_down[l][:, j * 128:(j + 1) * 128])))
        assert len(t) == SPL
        wsrc[l * SPL:(l + 1) * SPL] = np.stack(t)
    rows = np.zeros((2, 128, NROW), np.float32)
    cols = np.zeros((128, 2 * NCOL + 8), np.float32)
    lup = np.zeros((2, 128, NLUP), np.float32)
    for l in range(2):
        vecs = [f("k_k")[l], f("k_a")[l], f("r_k")[l].reshape(-1), f("ln_x_w")[l], f("ln_x_b")[l], f("mu_shift")[l]]
        rows[l] = np.broadcast_to(np.concatenate(vecs)[None, :], (128, NROW))
        c = l * NCOL
        cols[:, c:c + 8] = f("norm_mix_g")[l].reshape(8, 128).T
        cols[:, c + 8:c + 16] = f("norm_ffn_g")[l].reshape(8, 128).T
        cols[:, c + 16:c + 28] = f("conv_w")[l].reshape(3, 4, 128).transpose(2, 1, 0).reshape(128, 12)
        cols[:, c + 28:c + 160] = f("ffn_conv_w")[l].reshape(3, 44, 128).transpose(2, 1, 0).reshape(128, 132)
        cols[:, c + 160:c + 204] = f("ffn_conv_b")[l].reshape(44, 128).T
        lup[l, 0:64, 0:512] = f("decay_up")[l]
        lup[l, 64, 0:512] = f("w0")[l]
        lup[l, 0:64, 512:1024] = f("a_up")[l]
        lup[l, 64, 512:1024] = f("a0")[l]
        lup[l, :, 1024:1536] = f("g_up")[l]
        if l == 1:
            lup[l, 0:32, 1536:2048] = f("vres_up")[0]
            lup[l, 32, 1536:2048] = f("v0")[0]
            lup[l, :, 2048:2304] = _km(f("vres_down")[0]).reshape(128, 256)
    cols[:, 2 * NCOL:2 * NCOL + 8] = f("norm_final_g").reshape(8, 128).T
    brows = np.zeros((128, 2, 512), np.float32)
    for r, vec in enumerate([f("w0")[0], f("a0")[0], f("w0")[1], f("a0")[1], f("v0")[0]]):
        brows[32 * (r % 3), r // 3] = vec
    s_, t_ = np.arange(128)[:, None], np.arange(128)[None, :]
    cst = np.concatenate([np.eye(128), (s_ <= t_), (s_ < t_), (s_ > t_)], axis=1).astype(np.float32)
    return dict(wsrc=wsrc, rows=rows, cols=cols, lup=lup, cst=cst, brows=brows.reshape(128, 1024))


_NC_CACHE = {}


N_LAUNCH = 1


def kernel(**inputs):
    x = np.asarray(inputs["x"], np.float32)
    bsz = x.shape[0]
    per = bsz // N_CORES
    pl = per // N_LAUNCH
    shared = host_layout(inputs)
    key = (pl, SEQ // TG)
    if key not in _NC_CACHE:
        _NC_CACHE[key] = build_program(n_seq=pl, n_groups=SEQ // TG)
    nc = _NC_CACHE[key]
    out = np.zeros((bsz, SEQ, D), np.float32)
    for h in range(N_LAUNCH):
        in_maps = []
        for c in range(N_CORES):
            m = dict(shared)
            b0 = c * per + h * pl
            m["x"] = np.ascontiguousarray(x[b0:b0 + pl].reshape(pl * SEQ, D))
            in_maps.append(m)
        res = run_bass_kernel_spmd(nc, in_maps, core_ids=list(range(N_CORES)))
        for c in range(N_CORES):
            b0 = c * per + h * pl
            out[b0:b0 + pl] = np.asarray(res.results[c]["y"], np.float32).reshape(pl, SEQ, D)
    return out
```

```python
import math
import itertools
from contextlib import ExitStack

import numpy as np
import concourse.bass as bass
import concourse.mybir as mybir
from concourse.bass_utils import run_bass_kernel_spmd

F32 = mybir.dt.float32
BF16 = mybir.dt.bfloat16
AF = mybir.ActivationFunctionType
ALU = mybir.AluOpType
AX = mybir.AxisListType

N_CORES = 8
D = 1024
SEQ = 2048
TG = 512
NCH = TG // 128
DFF = 2816
NFC = DFF // 128
C0 = math.exp(-0.5)
RMS_EPS = 1e-6
GN_EPS = 64 * 1e-5
SLOT = 4096
NSLOT = 4
TPL = 39
SPL = 36
NROW = 5 * 512 + 1792
NCOL = 8 + 8 + 12 + 132 + 44
NLUP = 4 * 512 + 256 + 3 * 512
ARENA = 123008


class Sem:
    def __init__(self, h, owner):
        self.h, self.owner, self.count = h, owner, 0


class Buf:
    __slots__ = ("region", "lo", "hi", "writers", "readers", "name")

    def __init__(self, region, lo, hi, name=""):
        self.region, self.lo, self.hi, self.name = region, lo, hi, name
        self.writers, self.readers = {}, {}


class Trk:
    def __init__(self, nc, es):
        self.nc = nc
        self.eng = {"pe": nc.tensor, "act": nc.scalar, "dve": nc.vector, "pool": nc.gpsimd, "sp": nc.sync}
        self.esem = {k: Sem(es.enter_context(nc.semaphore("s_" + k)), k) for k in ("pe", "act", "dve", "pool")}
        self.es = es
        self.waited = {}
        self.regions = {}
        self.dsems = []
        self.ninst = 0

    def new_epoch(self):
        self.epoch = getattr(self, "epoch", 0) + 1
        for k in ("pe", "act", "dve", "pool"):
            self.esem[k] = Sem(self.es.enter_context(self.nc.semaphore("s_%s_%d" % (k, self.epoch))), k)

    def dsem(self, name):
        s = Sem(self.es.enter_context(self.nc.semaphore(name)), None)
        self.dsems.append(s)
        return s

    def buf(self, region, lo=0, hi=1 << 30, name=""):
        b = Buf(region, lo, hi, name)
        self.regions.setdefault(region, []).append(b)
        return b

    def _deps(self, engname, reads, writes):
        deps = {}

        def add(sem, val, raw):
            if sem.owner == engname and engname == "pe":
                return
            if deps.get(sem, 0) < val:
                deps[sem] = val

        for b in reads:
            for o in self.regions[b.region]:
                if o.lo < b.hi and b.lo < o.hi:
                    for s, v in o.writers.items():
                        add(s, v, True)
        for b in writes:
            for o in self.regions[b.region]:
                if o.lo < b.hi and b.lo < o.hi:
                    for s, v in o.writers.items():
                        add(s, v, False)
                    for s, v in o.readers.items():
                        add(s, v, False)
        e = self.eng[engname]
        for s, v in deps.items():
            key = (engname, s)
            if self.waited.get(key, 0) < v:
                e.wait_ge(s.h, v)
                self.waited[key] = v

    def _record(self, sem, val, reads, writes):
        for b in reads:
            b.readers[sem] = val
        for b in writes:
            b.readers = {}
            b.writers = {sem: val}

    def emit(self, engname, fn, reads, writes):
        self._deps(engname, reads, writes)
        inst = fn()
        s = self.esem[engname]
        s.count += 1
        inst.then_inc(s.h, 1)
        self._record(s, s.count, reads, writes)
        self.ninst += 1

    def dma(self, out, in_, sem, reads, writes):
        self._deps("sp", reads, writes)
        inst = self.nc.sync.dma_start(out=out, in_=in_)
        sem.count += 16
        inst.then_inc(sem.h, 16)
        self._record(sem, sem.count, reads, writes)
        self.ninst += 1


def build_program(n_seq=4, n_groups=SEQ // TG, n_layers=2):
    nc = bass.Bass("TRN2", target_bir_lowering=False)
    ntok = n_seq * n_groups * TG
    x_d = nc.dram_tensor("x", [ntok, D], F32, kind="ExternalInput").ap()
    wsrc_d = nc.dram_tensor("wsrc", [2 * SPL, 128, SLOT], F32, kind="ExternalInput").ap()
    rows_d = nc.dram_tensor("rows", [2, 128, NROW], F32, kind="ExternalInput").ap()
    cols_d = nc.dram_tensor("cols", [128, 2 * NCOL + 8], F32, kind="ExternalInput").ap()
    lup_d = nc.dram_tensor("lup", [2, 128, NLUP], F32, kind="ExternalInput").ap()
    cst_d = nc.dram_tensor("cst", [128, 4 * 128], F32, kind="ExternalInput").ap()
    brows_d = nc.dram_tensor("brows", [128, 1024], F32, kind="ExternalInput").ap()
    y_d = nc.dram_tensor("y", [ntok, D], F32, kind="ExternalOutput").ap()
    wbf_d = nc.dram_tensor("wbf", [2 * TPL, 128, SLOT], BF16, kind="Internal").ap()

    with ExitStack() as es:
        T = Trk(nc, es)

        def sb(name, shape, dt):
            return es.enter_context(nc.sbuf_tensor("sb_" + name, shape, dt))

        identf = sb("identf", [128, 128], F32)
        uiu_f = sb("uiu_f", [128, 128], F32)
        usu_f = sb("usu_f", [128, 128], F32)
        usl_f = sb("usl_f", [128, 128], F32)
        onescol = sb("onescol", [128, 2], F32)
        identb = sb("identb", [128, 128], BF16)
        onesb = sb("onesb", [128, 128], BF16)
        mask4 = sb("mask4", [128, 512], BF16)
        masksl = sb("masksl", [128, 128], BF16)
        rows = sb("rows", [128, 5, 512], F32)
        cols = sb("cols", [128, 2 * NCOL + 8], F32)
        dupx = sb("dupx", [128, 2, 512], BF16)
        aupx = sb("aupx", [128, 2, 512], BF16)
        gupb = sb("gupb", [128, 2, 512], BF16)
        vupx = sb("vupx", [128, 512], BF16)
        vdn = sb("vdn", [128, 8, 32], BF16)
        blo = sb("blo", [128, 2, 512], BF16)
        STf = sb("STf", [128, 2, 4, 128], F32)
        STb = sb("STb", [128, 2, 2, 4, 128], BF16)
        hprev = sb("hprev", [128, 2, 8], BF16)
        cutail = sb("cutail", [128, 2, 4, 2], F32)
        ztail = sb("ztail", [128, 2, 44, 2], F32)
        xT = sb("xT", [128, 8, TG], F32)
        ring = sb("ring", [128, NSLOT, SLOT], BF16)
        rstd = sb("rstd", [128, TG], F32)
        sqk = sb("sqk", [128, 2, TG], BF16)
        small = sb("small", [128, 64], F32)
        arena = sb("arena", [128, ARENA // 4], F32)
        ps = es.enter_context(nc.psum_tensor("ps", [128, 8, 512], F32))

        B = {}
        for nm in ("consts", "rows", "cols", "lora", "hprev", "cutail", "ztail", "rstd", "rsq", "small_wc", "small_wc1",
                   "small_a", "small_b", "small_c", "small_d", "small_e"):
            B[nm] = T.buf(nm)
        B_ST = [T.buf("STf%d" % l) for l in range(2)]
        B_STb = [[T.buf("STb%d_%d" % (l, p)) for p in range(2)] for l in range(2)]
        B_xT = [T.buf("xT%d" % k) for k in range(8)]
        B_sqk = [T.buf("sqk%d" % k) for k in range(2)]
        B_ring = [T.buf("ring%d" % k) for k in range(NSLOT)]
        B_ps = [T.buf("ps%d" % k) for k in range(8)]
        B_wbf = [T.buf("wbf%d" % k) for k in range(2 * TPL)]
        ring_sem = [T.dsem("rs%d" % k) for k in range(NSLOT)]
        ld_sem = T.dsem("ld")
        rows_sem = T.dsem("rowsld")
        st_sem = T.dsem("st")
        pin_sem = [T.dsem("pin%d" % k) for k in range(2)]
        pout_sem = [T.dsem("pout%d" % k) for k in range(4)]
        misc_sem = [T.dsem("misc%d" % k) for k in range(6)]

        def AB(off, nbytes, name=""):
            assert off % 4 == 0 and off + nbytes <= ARENA, (name, off, nbytes)
            return T.buf("arena", off, off + nbytes, name)

        def AV(off, dt, shape):
            n = int(np.prod(shape))
            if dt == F32:
                v = arena[:, off // 4: off // 4 + n]
            else:
                assert n % 2 == 0
                v = arena[:, off // 4: off // 4 + n // 2].bitcast(BF16)
            if len(shape) == 2:
                return v.rearrange("p (a b) -> p a b", a=shape[0])
            if len(shape) == 3:
                return v.rearrange("p (a b c) -> p a b c", a=shape[0], b=shape[1])
            if len(shape) == 4:
                return v.rearrange("p (a b c d) -> p a b c d", a=shape[0], b=shape[1], c=shape[2])
            return v

        o = 0
        hT = AV(o, BF16, [8, 514]); B_hT = AB(o, 8224, "hT"); o += 8224
        tw = AV(o, BF16, [512]); B_tw = AB(o, 1024); o += 1024
        adx = AV(o, BF16, [512]); B_adx = AB(o, 1024); o += 1024
        sgd = AV(o, BF16, [512]); B_sgd = AB(o, 1024); o += 1024
        hvx = AV(o, BF16, [512]); B_hvx = AB(o, 1024); o += 1024
        RAW_OFF = o
        raw = AV(o, F32, [3, NCH, 512]); B_raw = [[AB(o + (q * NCH + t) * 2048, 2048) for t in range(NCH)] for q in range(3)]; o += 3 * NCH * 2048
        yaT = AV(o, BF16, [4, TG]); B_yaT = AB(o, 4096); o += 4096
        ybT = AV(o, BF16, [4, TG]); B_ybT = [AB(o + j * 1024, 1024) for j in range(4)]; o += 4096
        TMP_OFF = o
        NTMP = 9
        tmp = [AV(o + i * 2048, F32, [512]) for i in range(NTMP)]
        B_tmp = [AB(o + i * 2048, 2048) for i in range(NTMP)]
        o += NTMP * 2048
        tmB = AV(o, BF16, [4, 512]); B_tmB = [AB(o + q * 1024, 1024) for q in range(4)]; o += 4096
        khat2 = [AV(o + i * 1024, BF16, [512]) for i in range(2)]; B_khat2 = [AB(o + i * 1024, 1024) for i in range(2)]; o += 2048
        bhat2 = [AV(o + i * 1024, BF16, [512]) for i in range(2)]; B_bhat2 = [AB(o + i * 1024, 1024) for i in range(2)]; o += 2048
        vbf2 = [AV(o + i * 1024, BF16, [512]) for i in range(2)]; B_vbf2 = [AB(o + i * 1024, 1024) for i in range(2)]; o += 2048
        tg2 = [AV(o + i * 2048, F32, [512]) for i in range(2)]; B_tg2 = [AB(o + i * 2048, 2048) for i in range(2)]; o += 4096
        tbv2 = [AV(o + i * 2048, F32, [512]) for i in range(2)]; B_tbv2 = [AB(o + i * 2048, 2048) for i in range(2)]; o += 4096
        FMPAD_OFF = o
        fmpad = AV(o, BF16, [4, 2, 2, 128]); B_fmpad = AB(o, 4096); o += 4096
        fmkb = AV(o, BF16, [4, 2, 128]); B_fmkb = AB(o, 2048); o += 2048
        AM = AV(o, BF16, [8, 512]); B_AM = [AB(o + h * 1024, 1024) for h in range(8)]; o += 8192
        PT = [AV(o + i * 2048, BF16, [8, 128]) for i in range(2)]
        B_PT = [[AB(o + i * 2048 + hf * 1024, 1024) for hf in range(2)] for i in range(2)]; o += 4096
        PY = [AV(o + i * 4096, BF16, [8, 2, 128]) for i in range(2)]
        B_PY = [[AB(o + i * 4096 + hf * 2048, 2048) for hf in range(2)] for i in range(2)]; o += 8192
        xtb = AV(o, BF16, [512]); B_xtb = AB(o, 1024); o += 1024
        satb = AV(o, BF16, [512]); B_satb = AB(o, 1024); o += 1024
        yatm = AV(o, BF16, [512]); B_yatm = AB(o, 1024); o += 1024
        VF_OFF = o
        vfirst = AV(o, F32, [NCH, 512]); B_vf = [AB(o + t * 2048, 2048) for t in range(NCH)]; o += NCH * 2048
        assert o <= ARENA, o
        o = RAW_OFF
        sga = [AV(o + i * 1024, BF16, [512]) for i in range(2)]; B_sga = [AB(o + i * 1024, 1024) for i in range(2)]; o += 2048
        sgb = [AV(o + i * 1024, BF16, [512]) for i in range(2)]; B_sgb = [AB(o + i * 1024, 1024) for i in range(2)]; o += 2048
        m1 = [AV(o + i * 2048, F32, [512]) for i in range(2)]; B_m1 = [AB(o + i * 2048, 2048) for i in range(2)]; o += 4096
        m2 = [AV(o + i * 2048, F32, [512]) for i in range(2)]; B_m2 = [AB(o + i * 2048, 2048) for i in range(2)]; o += 4096
        merged = AV(o, BF16, [8, TG]); B_merged = [AB(o + j * 1024, 1024) for j in range(8)]; o += 8192
        assert o <= RAW_OFF + 3 * NCH * 2048
        o = TMP_OFF
        usb = AV(o, F32, [512]); B_usb = AB(o, 2048); o += 2048
        cu = AV(o, F32, [514]); B_cu = AB(o, 2064); o += 2064
        cacc = AV(o, F32, [512]); B_cacc = AB(o, 2048); o += 2048
        o = 0
        h2T = AV(o, BF16, [8, TG]); B_h2T = AB(o, 8192); o += 8192
        actb = AV(o, BF16, [NFC, TG]); B_act = [AB(o + i * 1024, 1024) for i in range(NFC)]; o += NFC * 1024
        zg = [AV(o + i * 2064, F32, [514]) for i in range(2)]; B_zg = [AB(o + i * 2064, 2064) for i in range(2)]; o += 4128
        zu = [AV(o + i * 2064, F32, [514]) for i in range(2)]; B_zu = [AB(o + i * 2064, 2064) for i in range(2)]; o += 4128
        ag = [AV(o + i * 2048, F32, [512]) for i in range(2)]; B_ag = [AB(o + i * 2048, 2048) for i in range(2)]; o += 4096
        au = [AV(o + i * 2048, F32, [512]) for i in range(2)]; B_au = [AB(o + i * 2048, 2048) for i in range(2)]; o += 4096
        sgl = [AV(o + i * 2048, F32, [512]) for i in range(2)]; B_sgl = [AB(o + i * 2048, 2048) for i in range(2)]; o += 4096
        assert o <= VF_OFF
        xtm = AV(RAW_OFF, F32, [4, 1024]); B_xtm = AB(RAW_OFF, 16384, "xtm")
        xin = AV(RAW_OFF + 16384, F32, [4, 1024]); B_xin = AB(RAW_OFF + 16384, 16384, "xin")
        assert RAW_OFF + 32768 <= TMP_OFF
        yTf = AV(TMP_OFF, F32, [8, TG]); B_yT = [AB(TMP_OFF + k * 2048, 2048) for k in range(8)]
        pin = [AV(i * 16384, F32, [SLOT]) for i in range(2)]; B_pin = [AB(i * 16384, 16384) for i in range(2)]
        pout = [AV(32768 + i * 8192, BF16, [SLOT]) for i in range(4)]; B_pout = [AB(32768 + i * 8192, 8192) for i in range(4)]
        murow = AV(65536, F32, [1792]); B_mu = AB(65536, 7168)
        omrow = AV(65536 + 7168, F32, [1792]); B_om = AB(65536 + 7168, 7168)
        lupst = AV(81920, F32, [NLUP]); B_lupst = AB(81920, NLUP * 4)
        lotmp = AV(81920 + NLUP * 4, F32, [512]); B_lotmp = AB(81920 + NLUP * 4, 2048)
        cstst = AV(100352, F32, [512]); B_cstst = AB(100352, 2048)

        def mm(out, lhsT, rhs, start, stop, R, W):
            T.emit("pe", lambda: nc.tensor.matmul(out, lhsT=lhsT, rhs=rhs, start=start, stop=stop), R, W)

        def tr(out, in_, ident, R, W):
            T.emit("pe", lambda: nc.tensor.transpose(out=out, in_=in_, identity=ident), R, W)

        def act(out, in_, func, R, W, scale=1.0, bias=0.0):
            T.emit("act", lambda: nc.scalar.activation(out=out, in_=in_, func=func, scale=scale, bias=bias), R, W)

        def cp(eng, out, in_, R, W):
            if eng == "act":
                T.emit("act", lambda: nc.scalar.copy(out=out, in_=in_), R, W)
            else:
                T.emit(eng, lambda: T.eng[eng].tensor_copy(out=out, in_=in_), R, W)

        def tt(eng, out, in0, in1, op, R, W):
            T.emit(eng, lambda: T.eng[eng].tensor_tensor(out=out, in0=in0, in1=in1, op=op), R, W)

        def ts(eng, out, in0, s1, s2, op0, op1, R, W):
            if s2 is None:
                T.emit(eng, lambda: T.eng[eng].tensor_scalar(out=out, in0=in0, scalar1=s1, scalar2=None, op0=op0), R, W)
            else:
                T.emit(eng, lambda: T.eng[eng].tensor_scalar(out=out, in0=in0, scalar1=s1, scalar2=s2, op0=op0, op1=op1), R, W)

        def stt(out, in0, scalar, in1, op0, op1, R, W):
            T.emit("dve", lambda: nc.vector.scalar_tensor_tensor(out=out, in0=in0, scalar=scalar, in1=in1, op0=op0, op1=op1), R, W)

        def red(out, in_, R, W):
            T.emit("dve", lambda: nc.vector.tensor_reduce(out=out, in_=in_, axis=AX.X, op=ALU.add), R, W)

        def recip(out, in_, R, W):
            T.emit("dve", lambda: nc.vector.reciprocal(out=out, in_=in_), R, W)

        def mset(eng, ap, val, W):
            T.emit(eng, lambda: T.eng[eng].memset(ap, val), [], W)

        pbank = [0]

        def nextbank():
            b = pbank[0]
            pbank[0] = (b + 1) % 6
            return b + 2

        pbank1 = [0]

        def nextbank_s1():
            pbank1[0] ^= 1
            return pbank1[0]

        rr = [0]

        def vp():
            rr[0] += 1
            return "pool" if rr[0] % 3 == 0 else "dve"

        ca = [0]

        def av():
            ca[0] += 1
            return "act" if ca[0] % 2 == 0 else "dve"

        T.dma(cstst, cst_d, misc_sem[0], [], [B_cstst])
        cp("dve", identf[:], cstst[:, 0:128], [B_cstst], [B["consts"]])
        cp("dve", uiu_f[:], cstst[:, 128:256], [B_cstst], [B["consts"]])
        cp("dve", usu_f[:], cstst[:, 256:384], [B_cstst], [B["consts"]])
        cp("dve", usl_f[:], cstst[:, 384:512], [B_cstst], [B["consts"]])
        cp("dve", identb[:], cstst[:, 0:128], [B_cstst], [B["consts"]])
        cp("dve", masksl[:], cstst[:, 384:512], [B_cstst], [B["consts"]])
        for q in range(4):
            src = cstst[:, 256:384] if q % 2 == 0 else cstst[:, 128:256]
            cp("dve", mask4[:, q * 128:(q + 1) * 128], src, [B_cstst], [B["consts"]])
        mset("pool", onesb[:], 1.0, [B["consts"]])
        mset("pool", onescol[:], 1.0, [B["consts"]])
        T.dma(cols[:], cols_d, misc_sem[1], [], [B["cols"]])
        T.dma(lupst[:, 0:1024], brows_d, misc_sem[4], [], [B_lupst])
        cp("dve", blo[:].rearrange("p a n -> p (a n)"), lupst[:, 0:1024], [B_lupst], [B["lora"]])
        cp("dve", lupst[:, 1024:2048], blo[:].rearrange("p a n -> p (a n)"), [B["lora"]], [B_lupst])
        tt("dve", lupst[:, 1024:2048], lupst[:, 0:1024], lupst[:, 1024:2048], ALU.subtract, [B_lupst], [B_lupst])
        cp("dve", blo[:].rearrange("p a n -> p (a n)"), lupst[:, 1024:2048], [B_lupst], [B["lora"]])
        for l in range(2):
            T.dma(lupst, lup_d[l], misc_sem[3], [B_lupst], [B_lupst])
            R_, W_ = [B_lupst], [B["lora"]]
            cp("dve", dupx[:, l, :], lupst[:, 0:512], R_, W_)
            cp("dve", aupx[:, l, :], lupst[:, 512:1024], R_, W_)
            cp("dve", gupb[:, l, :], lupst[:, 1024:1536], R_, W_)
            if l == 1:
                cp("dve", vupx[:], lupst[:, 1536:2048], R_, W_)
                cp("dve", vdn[:].rearrange("p a b -> p (a b)"), lupst[:, 2048:2304], R_, W_)
        def src_to_stream(l, s):
            if s == 0:
                return [(0, "lora")]
            if 1 <= s <= 3:
                return [(1 + 2 * (s - 1), "a"), (2 + 2 * (s - 1), "b")]
            if 4 <= s <= 6:
                return [(7 + s - 4, "c")]
            if 7 <= s <= 14:
                return [(10 + s - 7, "c")]
            if 15 <= s <= 16:
                return [(18 + s - 15, "c")]
            if 17 <= s <= 27:
                return [(20 + s - 17, "c")]
            return [(31 + s - 28, "c")]

        pi, po = 0, 0
        for l in range(n_layers):
            T.dma(murow, rows_d[l, :, 2560:2560 + 1792], misc_sem[5], [B_mu, B_om], [B_mu])
            ts("dve", omrow, murow, -1.0, 1.0, ALU.mult, ALU.add, [B_mu], [B_om])
            for s in range(SPL):
                bi_ = pi % 2
                pi += 1
                T.dma(pin[bi_], wsrc_d[l * SPL + s], pin_sem[bi_], [], [B_pin[bi_]])
                for (st, kind) in src_to_stream(l, s):
                    bo = po % 4
                    po += 1
                    eng = ("dve", "act", "pool")[po % 3]
                    if kind == "c":
                        cp(eng, pout[bo], pin[bi_], [B_pin[bi_]], [B_pout[bo]])
                    elif kind == "lora":
                        w3 = pin[bi_][:, 0:2048].rearrange("p (k n) -> p k n", k=8)
                        mu_b = murow[:, 1536:1792].unsqueeze(1).to_broadcast([128, 8, 256])
                        om_b = omrow[:, 1536:1792].unsqueeze(1).to_broadcast([128, 8, 256])
                        tt("dve", pout[bo][:, 0:2048].rearrange("p (k n) -> p k n", k=8), w3, om_b, ALU.mult,
                           [B_pin[bi_], B_om], [B_pout[bo]])
                        tt("dve", pout[bo][:, 2048:4096].rearrange("p (k n) -> p k n", k=8), w3, mu_b, ALU.mult,
                           [B_pin[bi_], B_mu], [B_pout[bo]])
                    else:
                        q = (st - 1) // 2
                        w3 = pin[bi_].rearrange("p (k n) -> p k n", k=8)
                        rowsrc = omrow if kind == "a" else murow
                        m_b = rowsrc[:, q * 512:(q + 1) * 512].unsqueeze(1).to_broadcast([128, 8, 512])
                        tt("dve" if kind == "a" else "pool", pout[bo].rearrange("p (k n) -> p k n", k=8), w3, m_b, ALU.mult,
                           [B_pin[bi_], B_mu, B_om], [B_pout[bo]])
                    T.dma(wbf_d[l * TPL + st], pout[bo], pout_sem[bo], [B_pout[bo]], [B_wbf[l * TPL + st]])

        for s_ in T.dsems:
            if s_.count:
                nc.sync.wait_ge(s_.h, s_.count)
                T.waited[("sp", s_)] = s_.count
        for b_ in B_wbf:
            b_.writers = {}
        mset("pool", arena[:, FMPAD_OFF // 4:FMPAD_OFF // 4 + 1024], 0.0, [B_fmpad])

        order = []
        for g_ in range(n_seq * n_groups):
            for l in range(n_layers):
                order += [l * TPL + k for k in range(TPL)]
        sstate = {"next_load": 0, "next_use": 0}

        def _load(i):
            slot = i % NSLOT
            T.dma(ring[:, slot, :], wbf_d[order[i]], ring_sem[slot], [B_wbf[order[i]]], [B_ring[slot]])

        def acquire(expect):
            i = sstate["next_use"]
            assert order[i] % TPL == expect, (order[i] % TPL, expect)
            if i == 0:
                for j in range(min(NSLOT - 1, len(order))):
                    _load(j)
                sstate["next_load"] = min(NSLOT - 1, len(order))
            nl = sstate["next_load"]
            if nl < len(order) and nl <= i + NSLOT - 1 and i >= 1:
                _load(nl)
                sstate["next_load"] = nl + 1
            sstate["next_use"] = i + 1
            slot = i % NSLOT
            return ring[:, slot, :], B_ring[slot]

        class ChunkStream:
            def __init__(self, first_tile):
                self.t, self.c, self.cur = first_tile, 0, None

            def next(self):
                if self.c % 4 == 0:
                    self.cur = acquire(self.t)
                    self.t += 1
                v = self.cur[0].rearrange("p (c k n) -> p c k n", c=4, k=8)[:, self.c % 4]
                self.c += 1
                return v, self.cur[1]

        def rmsnorm(gcol, dst_fn, B_dst_fn):
            bss = nextbank()
            for kc in range(8):
                k2 = kc % 2
                act(sqk[:, k2, :], xT[:, kc, :], AF.Square, [B_xT[kc]], [B_sqk[k2]])
                mm(ps[:, bss, :], onesb[:], sqk[:, k2, :], kc == 0, kc == 7, [B_sqk[k2], B["consts"]], [B_ps[bss]])
            act(rstd[:], ps[:, bss, :], AF.Sqrt, [B_ps[bss]], [B["rstd"]], scale=1.0 / D, bias=RMS_EPS)
            recip(rstd[:], rstd[:], [B["rstd"]], [B["rstd"]])
            for kc in range(8):
                stt(dst_fn(kc), xT[:, kc, :], cols[:, gcol + kc:gcol + kc + 1], rstd[:], ALU.mult, ALU.mult,
                    [B_xT[kc], B["cols"], B["rstd"]], [B_dst_fn(kc)])

        def proj_fm(chunk, Bch, rhs_fn, R_rhs, nk=8):
            b = nextbank()
            for kc in range(nk):
                mm(ps[:, b, :], chunk[:, kc, :], rhs_fn(kc), kc == 0, kc == nk - 1, [Bch] + R_rhs, [B_ps[b]])
            return b

        def mixer(l, first_group):
            co = l * NCOL
            T.dma(rows[:], rows_d[l, :, 0:2560].rearrange("p (a b) -> p a b", a=5), rows_sem, [], [B["rows"]])
            cp("pool", hT[:, :, 1:2], hprev[:, l, :].unsqueeze(2), [B["hprev"]], [B_hT])
            rmsnorm(co, lambda kc: hT[:, kc, 2:514], lambda kc: B_hT)
            cp("pool", hprev[:, l, :].unsqueeze(2), hT[:, :, 513:514], [B_hT], [B["hprev"]])
            ha = lambda kc: hT[:, kc, 2:514]
            hb = lambda kc: hT[:, kc, 1:513]
            mset("pool", tw[64:65, :], 1.0, [B_tw])
            mset("pool", adx[64:65, :], 1.0, [B_adx])
            if l == 1:
                mset("pool", hvx[32:33, :], 1.0, [B_hvx])
            lt, Blt = acquire(0)
            la = lt[:, 0:2048].rearrange("p (k n) -> p k n", k=8)
            lb = lt[:, 2048:4096].rearrange("p (k n) -> p k n", k=8)
            for (c0, c1, dst, Bdst, fn) in ((0, 64, tw, B_tw, AF.Tanh), (64, 128, adx, B_adx, AF.Copy), (128, 256, sgd, B_sgd, AF.Sigmoid)):
                b = nextbank()
                m = c1 - c0
                for kc in range(8):
                    mm(ps[0:m, b, :], la[:, kc, c0:c1], ha(kc), kc == 0, False, [Blt, B_hT], [B_ps[b]])
                    mm(ps[0:m, b, :], lb[:, kc, c0:c1], hb(kc), False, kc == 7, [Blt, B_hT], [B_ps[b]])
                act(dst[0:m, :], ps[0:m, b, :], fn, [B_ps[b]], [Bdst])
            if l == 1:
                b = nextbank()
                for kc in range(8):
                    mm(ps[0:32, b, :], vdn[:, kc, :], ha(kc), kc == 0, kc == 7, [B["lora"], B_hT], [B_ps[b]])
                act(hvx[0:32, :], ps[0:32, b, :], AF.Copy, [B_ps[b]], [B_hvx])
            for q in range(3):
                wa, Bwa = acquire(1 + 2 * q)
                wb_, Bwb = acquire(2 + 2 * q)
                wa3 = wa.rearrange("p (k n) -> p k n", k=8)
                wb3 = wb_.rearrange("p (k n) -> p k n", k=8)
                for t in range(NCH):
                    b = nextbank()
                    for kc in range(8):
                        mm(ps[:, b, :], hT[:, kc, 2 + t * 128:2 + (t + 1) * 128], wa3[:, kc, :], kc == 0, False, [Bwa, B_hT], [B_ps[b]])
                        mm(ps[:, b, :], hT[:, kc, 1 + t * 128:1 + (t + 1) * 128], wb3[:, kc, :], False, kc == 7, [Bwb, B_hT], [B_ps[b]])
                    if q == 2 and l == 0:
                        cp(av(), vfirst[:, t, :], ps[:, b, :], [B_ps[b]], [B_vf[t]])
                    else:
                        cp(av(), raw[:, q, t, :], ps[:, b, :], [B_ps[b]], [B_raw[q][t]])
            cs = ChunkStream(7)
            cwo = co + 16
            for j in range(4):
                ch, Bch = cs.next()
                b = proj_fm(ch, Bch, ha, [B_hT])
                cp("act", usb, ps[:, b, :], [B_ps[b]], [B_usb])
                ch, Bch = cs.next()
                b = proj_fm(ch, Bch, ha, [B_hT])
                cp("pool", cu[:, 0:2], cutail[:, l, j, :], [B["cutail"]], [B_cu])
                tt("dve", cu[:, 2:514], ps[:, b, :], usb, ALU.mult, [B_ps[b], B_usb], [B_cu])
                cp("pool", cutail[:, l, j, :], cu[:, 512:514], [B_cu], [B["cutail"]])
                w = lambda k_: cols[:, cwo + j * 3 + k_:cwo + j * 3 + k_ + 1]
                ts("dve", cacc, cu[:, 0:512], w(0), None, ALU.mult, ALU.bypass, [B_cu, B["cols"]], [B_cacc])
                stt(cacc, cu[:, 1:513], w(1), cacc, ALU.mult, ALU.add, [B_cu, B_cacc, B["cols"]], [B_cacc])
                stt(cacc, cu[:, 2:514], w(2), cacc, ALU.mult, ALU.add, [B_cu, B_cacc, B["cols"]], [B_cacc])
                ch, Bch = cs.next()
                b = proj_fm(ch, Bch, ha, [B_hT])
                tt("dve", ybT[:, j, :], ps[:, b, :], cacc, ALU.mult, [B_ps[b], B_cacc], [B_ybT[j]])
            g0 = wkv_s1(l, 0)
            for _ in g0:
                pass
            pend = None
            for t in range(NCH):
                gens = [g_ for g_ in (pend, wkv_s1(l, t + 1) if t + 1 < NCH else None) if g_ is not None]
                filler = itertools.chain(*gens)
                wkv_s2(l, t, filler)
                for _ in filler:
                    pass
                wkv_s3a(l, t)
                pend = wkv_s3b(l, t)
            for _ in pend:
                pass
            for j in range(8):
                i2 = j % 2
                mp, Bmp = acquire(10 + j)
                mp3 = mp[:, 0:3072].rearrange("p (k n) -> p k n", k=24)
                b = nextbank()
                for kc in range(8):
                    mm(ps[:, b, :], mp3[:, kc, :], ha(kc), kc == 0, kc == 7, [Bmp, B_hT], [B_ps[b]])
                act(sga[i2], ps[:, b, :], AF.Sigmoid, [B_ps[b]], [B_sga[i2]])
                b = nextbank()
                for kc in range(4):
                    mm(ps[:, b, :], mp3[:, 16 + kc, :], yaT[:, kc, :], kc == 0, kc == 3, [Bmp, B_yaT], [B_ps[b]])
                tt("dve", m1[i2], ps[:, b, :], sga[i2], ALU.mult, [B_ps[b], B_sga[i2]], [B_m1[i2]])
                b = nextbank()
                for kc in range(8):
                    mm(ps[:, b, :], mp3[:, 8 + kc, :], ha(kc), kc == 0, kc == 7, [Bmp, B_hT], [B_ps[b]])
                act(sgb[i2], ps[:, b, :], AF.Sigmoid, [B_ps[b]], [B_sgb[i2]])
                b = nextbank()
                for kc in range(4):
                    mm(ps[:, b, :], mp3[:, 20 + kc, :], ybT[:, kc, :], kc == 0, kc == 3, [Bmp] + B_ybT, [B_ps[b]])
                tt("dve", m2[i2], ps[:, b, :], sgb[i2], ALU.mult, [B_ps[b], B_sgb[i2]], [B_m2[i2]])
                tt("pool", merged[:, j, :], m1[i2], m2[i2], ALU.add, [B_m1[i2], B_m2[i2]], [B_merged[j]])
            cs = ChunkStream(18)
            for j in range(8):
                ch, Bch = cs.next()
                b = proj_fm(ch, Bch, lambda kc: merged[:, kc, :], B_merged)
                tt("dve", xT[:, j, :], xT[:, j, :], ps[:, b, :], ALU.add, [B_xT[j], B_ps[b]], [B_xT[j]])

        def wkv_s1(l, t):
            tok = slice(t * 128, (t + 1) * 128)
            db = t % 2
            khat, bhat, vbf = khat2[db], bhat2[db], vbf2[db]
            B_khat, B_bhat, B_vbf = B_khat2[db], B_bhat2[db], B_vbf2[db]
            r_t, k_t = raw[:, 0, t, :], raw[:, 1, t, :]
            Br, Bk = B_raw[0][t], B_raw[1][t]
            row = lambda i: rows[:, i, :]
            (t_sw, t_a, t_v, t_kap, t_kp, t_nb, t_e0, t_e1, t_x) = tmp
            (Bsw, Ba, Bv, Bkap, Bkp, Bnb, Be0, Be1, Bx) = B_tmp
            t_g, Bg, t_bv, Bbv = tg2[db], B_tg2[db], tbv2[db], B_tbv2[db]
            wcn = "small_wc" if db == 0 else "small_wc1"
            LO, CO_ = B["lora"], B["consts"]
            b = nextbank_s1()
            mm(ps[:, b, :], tw[0:65, tok], dupx[0:65, l, :], True, False, [B_tw, LO], [B_ps[b]])
            r_ = l * 2
            mm(ps[:, b, :], onesb[32 * (r_ % 3):32 * (r_ % 3) + 1, :], blo[32 * (r_ % 3):32 * (r_ % 3) + 1, r_ // 3, :], False, True, [CO_, LO], [B_ps[b]])
            act(t_sw, ps[:, b, :], AF.Sigmoid, [B_ps[b]], [Bsw])
            yield
            b = nextbank_s1()
            mm(ps[:, b, :], adx[0:65, tok], aupx[0:65, l, :], True, False, [B_adx, LO], [B_ps[b]])
            r_ = l * 2 + 1
            mm(ps[:, b, :], onesb[32 * (r_ % 3):32 * (r_ % 3) + 1, :], blo[32 * (r_ % 3):32 * (r_ % 3) + 1, r_ // 3, :], False, True, [CO_, LO], [B_ps[b]])
            act(t_a, ps[:, b, :], AF.Sigmoid, [B_ps[b]], [Ba])
            yield
            b = nextbank_s1()
            mm(ps[:, b, :], sgd[:, tok], gupb[:, l, :], True, True, [B_sgd, LO], [B_ps[b]])
            cp("act", t_g, ps[:, b, :], [B_ps[b]], [Bg])
            yield
            if l == 1:
                b = nextbank_s1()
                mm(ps[:, b, :], hvx[0:33, tok], vupx[0:33, :], True, False, [B_hvx, LO], [B_ps[b]])
                r_ = 4
                mm(ps[:, b, :], onesb[32 * (r_ % 3):32 * (r_ % 3) + 1, :], blo[32 * (r_ % 3):32 * (r_ % 3) + 1, r_ // 3, :], False, True, [CO_, LO], [B_ps[b]])
                act(t_x, ps[:, b, :], AF.Sigmoid, [B_ps[b]], [Bx])
                v_raw = raw[:, 2, t, :]
                tt("dve", t_v, vfirst[:, t, :], v_raw, ALU.subtract, [B_vf[t], B_raw[2][t]], [Bv])
                tt("pool", t_v, t_v, t_x, ALU.mult, [Bv, Bx], [Bv])
                tt("pool", t_v, t_v, v_raw, ALU.add, [Bv, B_raw[2][t]], [Bv])
                v_eff, Bve = t_v, Bv
            else:
                v_eff, Bve = vfirst[:, t, :], B_vf[t]
            WC = small[:, 48 + db * 8:48 + db * 8 + 8]
            tt("dve", t_kap, k_t, row(0), ALU.mult, [Bk, B["rows"]], [Bkap])
            yield
            tt("pool", t_x, t_kap, t_kap, ALU.mult, [Bkap], [Bx])
            yield
            ss = small[:, 8:16]
            red(ss, t_x.rearrange("p (h n) -> p h n", h=8), [Bx], [B["small_a"]])
            yield
            act(ss, ss, AF.Sqrt, [B["small_a"]], [B["small_a"]])
            yield
            ts("dve", ss, ss, 1e-12, None, ALU.max, ALU.bypass, [B["small_a"]], [B["small_a"]])
            yield
            recip(ss, ss, [B["small_a"]], [B["small_a"]])
            yield
            k3 = t_kap.rearrange("p (h n) -> p h n", h=8)
            tt("dve", k3, k3, ss.unsqueeze(2).to_broadcast([128, 8, 64]), ALU.mult, [Bkap, B["small_a"]], [Bkap])
            yield
            stt(t_kp, t_a, -1.0, row(1), ALU.add, ALU.mult, [Ba, B["rows"]], [Bkp])
            yield
            stt(t_kp, t_kp, 1.0, k_t, ALU.add, ALU.mult, [Bkp, Bk], [Bkp])
            yield
            stt(t_nb, t_a, -1.0, t_kap, ALU.mult, ALU.mult, [Ba, Bkap], [Bnb])
            yield
            bce = nextbank_s1()
            mm(ps[:, bce, :], usu_f[:], t_sw, True, True, [CO_, Bsw], [B_ps[bce]])
            act(t_e0, ps[:, bce, :], AF.Exp, [B_ps[bce]], [Be0], scale=-C0)
            yield
            tt("dve", tmB[:, 0, :], t_kap, t_e0, ALU.mult, [Bkap, Be0], [B_tmB[0]])
            bci = nextbank_s1()
            mm(ps[:, bci, :], uiu_f[:], t_sw, True, True, [CO_, Bsw], [B_ps[bci]])
            act(t_e1, ps[:, bci, :], AF.Exp, [B_ps[bci]], [Be1], scale=-C0)
            yield
            tt("pool", tmB[:, 1, :], r_t, t_e1, ALU.mult, [Br, Be1], [B_tmB[1]])
            act(t_e0, ps[:, bci, :], AF.Exp, [B_ps[bci]], [Be0], scale=C0)
            yield
            tt("dve", tmB[:, 2, :], t_kp, t_e0, ALU.mult, [Bkp, Be0], [B_tmB[2]])
            tt("pool", tmB[:, 3, :], t_nb, t_e0, ALU.mult, [Bnb, Be0], [B_tmB[3]])
            yield
            brv = nextbank_s1()
            mm(ps[:, brv, :], usl_f[:], t_sw, True, True, [CO_, Bsw], [B_ps[brv]])
            act(t_e1, ps[:, brv, :], AF.Exp, [B_ps[brv]], [Be1], scale=-C0)
            yield
            tt("dve", khat, t_kp, t_e1, ALU.mult, [Bkp, Be1], [B_khat])
            yield
            tt("pool", bhat, t_nb, t_e1, ALU.mult, [Bnb, Be1], [B_bhat])
            yield
            yield
            btot = nextbank_s1()
            for p in range(4):
                mm(ps[:, btot, 2 * p:2 * p + 2], t_sw[:, p * 128:(p + 1) * 128], onescol[:, 0:2], True, True, [CO_, Bsw], [B_ps[btot]])
            act(WC, ps[:, btot, 0:8], AF.Exp, [B_ps[btot]], [B[wcn]], scale=-C0)
            yield
            yield
            cp("pool", vbf, v_eff, [Bve], [B_vbf])
            yield
            tt("dve", t_x, r_t, t_kp, ALU.mult, [Br, Bkp], [Bx])
            yield
            tt("pool", t_x, t_x, row(2), ALU.mult, [Bx, B["rows"]], [Bx])
            yield
            bs = small[:, 16:24]
            red(bs, t_x.rearrange("p (h n) -> p h n", h=8), [Bx], [B["small_b"]])
            yield
            tt("dve", t_bv.rearrange("p (h n) -> p h n", h=8), v_eff.rearrange("p (h n) -> p h n", h=8),
               bs.unsqueeze(2).to_broadcast([128, 8, 64]), ALU.mult, [Bve, B["small_b"]], [Bbv])

        def wkv_s2(l, t, filler):
            CO_ = B["consts"]

            def fill(n):
                if filler is None:
                    return
                for _ in range(n):
                    try:
                        next(filler)
                    except StopIteration:
                        return
            for half in range(2):
                b = nextbank()
                psb = ps[:, b, :].bitcast(BF16)
                for pp in range(2):
                    p = half * 2 + pp
                    for q in range(4):
                        tr(psb[:, (pp * 4 + q) * 128:(pp * 4 + q + 1) * 128], tmB[:, q, p * 128:(p + 1) * 128], identb[:],
                           [B_tmB[q], CO_], [B_ps[b]])
                v4 = psb.rearrange("p (a q n) -> p a q n", a=2, q=4)
                for hh in range(2):
                    cp("act" if hh == 0 else "dve", fmpad[hh * 64:(hh + 1) * 64, half * 2:half * 2 + 2, hh, :, :],
                       v4[hh * 64:(hh + 1) * 64, :, 0:2, :], [B_ps[b]], [B_fmpad])
                cp("act", fmkb[:, half * 2:half * 2 + 2, :, :], v4[:, :, 2:4, :], [B_ps[b]], [B_fmkb])
                fill(2)
            for h in range(8):
                p, hh = h // 2, h % 2
                b = nextbank()
                rhs = fmpad[:, p, hh, :, :].rearrange("p a n -> p (a n)")
                mm(ps[:, b, 0:256], fmkb[:, p, 0, :], rhs, True, True, [B_fmkb, B_fmpad], [B_ps[b]])
                mm(ps[:, b, 256:512], fmkb[:, p, 1, :], rhs, True, True, [B_fmkb, B_fmpad], [B_ps[b]])
                tt("dve", AM[:, h, :], ps[:, b, :], mask4[:], ALU.mult, [B_ps[b], CO_], [B_AM[h]])
                if h % 2 == 1:
                    fill(1)
            for half in range(2):
                b = nextbank()
                for j in range(4):
                    h = half * 4 + j
                    p, hh = h // 2, h % 2
                    mm(ps[:, b, j * 128:(j + 1) * 128], fmpad[:, p, hh, 0, :], fmkb[:, p, 1, :], True, True, [B_fmkb, B_fmpad], [B_ps[b]])
                tt("dve", PT[0][:, half * 4:half * 4 + 4, :], ps[:, b, :].rearrange("p (a n) -> p a n", a=4),
                   masksl[:].unsqueeze(1).to_broadcast([128, 4, 128]), ALU.mult, [B_ps[b], CO_], [B_PT[0][half]])
                hs = slice(half * 4, half * 4 + 4)
                cp("pool", PY[0][:, hs, 0, :], AM[:, hs, 256:384], B_AM[half * 4:half * 4 + 4], [B_PY[0][half]])
                tt("pool", PY[1][:, hs, 1, :], AM[:, hs, 256:384], identb[:].unsqueeze(1).to_broadcast([128, 4, 128]), ALU.add,
                   B_AM[half * 4:half * 4 + 4] + [CO_], [B_PY[1][half]])
            for kk in range(1, 8):
                cur, nxt = (kk - 1) % 2, kk % 2
                for half in range(2):
                    Bc = [B_PT[cur][half], B_PY[cur][half]]
                    if kk == 1:
                        Bc = Bc
                    bA, bB = nextbank(), nextbank()
                    bC = nextbank() if kk <= 6 else None
                    for j in range(4):
                        h = half * 4 + j
                        bank = bA if j < 2 else bB
                        off = (j % 2) * 256
                        if kk == 1:
                            mm(ps[:, bank, off:off + 128], PT[cur][:, h, :], PY[cur][:, h, 0, :], True, True, Bc, [B_ps[bank]])
                        elif kk <= 6:
                            mm(ps[:, bank, off:off + 256], PT[cur][:, h, :], PY[cur][:, h, :, :].rearrange("p a n -> p (a n)"), True, True, Bc, [B_ps[bank]])
                        else:
                            mm(ps[:, bank, off + 128:off + 256], PT[cur][:, h, :], PY[cur][:, h, 1, :], True, True, Bc, [B_ps[bank]])
                        if kk <= 6:
                            mm(ps[:, bC, j * 128:(j + 1) * 128], PY[cur][:, h, 0, :], PT[cur][:, h, :], True, True, Bc, [B_ps[bC]])
                    for jj, bank in enumerate((bA, bB)):
                        hs = slice(half * 4 + jj * 2, half * 4 + jj * 2 + 2)
                        v3 = ps[:, bank, :].rearrange("p (a q n) -> p a q n", a=2, q=2)
                        if kk <= 6:
                            cp("act", PY[nxt][:, hs, 0, :], v3[:, :, 0, :], [B_ps[bank]], [B_PY[nxt][half]])
                        if kk >= 2:
                            tt("dve", PY[nxt][:, hs, 1, :], v3[:, :, 1, :], PY[cur][:, hs, 1, :], ALU.add, [B_ps[bank], B_PY[cur][half]], [B_PY[nxt][half]])
                    if kk <= 6:
                        cp("act", PT[nxt][:, half * 4:half * 4 + 4, :], ps[:, bC, :].rearrange("p (a n) -> p a n", a=4), [B_ps[bC]], [B_PT[nxt][half]])
                    fill(4)
        def wkv_s3a(l, t):
            par = t % 2
            db = t % 2
            khat, bhat, vbf = khat2[db], bhat2[db], vbf2[db]
            B_khat, B_bhat, B_vbf = B_khat2[db], B_bhat2[db], B_vbf2[db]
            (t_sw, t_a, t_v, t_kap, t_kp, t_nb, t_e0, t_e1, t_x) = tmp
            (Bsw, Ba, Bv, Bkap, Bkp, Bnb, Be0, Be1, Bx) = B_tmp
            t_g, Bg, t_bv, Bbv = tg2[db], B_tg2[db], tbv2[db], B_tbv2[db]
            wcn = "small_wc" if db == 0 else "small_wc1"
            WC = small[:, 48 + db * 8:48 + db * 8 + 8]
            row = lambda i: rows[:, i, :]
            CO_ = B["consts"]
            fin = 7 % 2
            Tm = lambda h: PY[fin][:, h, 1, :]
            BT = B_PY[fin]
            stb_old, Bst_old = STb[:, l, par], B_STb[l][par]
            bX = nextbank()
            for h in range(8):
                p, hh = h // 2, h % 2
                hc = slice(h * 64, (h + 1) * 64)
                mm(ps[:, bX, hc], fmpad[:, p, hh, 0, :], stb_old[:, p, hh * 64:(hh + 1) * 64], True, False, [B_fmpad, Bst_old], [B_ps[bX]])
                mm(ps[:, bX, hc], AM[:, h, 0:128], vbf[:, hc], False, True, [B_AM[h], B_vbf], [B_ps[bX]])
            cp("act", xtb, ps[:, bX, :], [B_ps[bX]], [B_xtb])
            bS = nextbank()
            for h in range(8):
                hc = slice(h * 64, (h + 1) * 64)
                mm(ps[:, bS, hc], Tm(h), xtb[:, hc], True, True, [BT[h // 4], B_xtb], [B_ps[bS]])
            cp("dve", satb, ps[:, bS, :], [B_ps[bS]], [B_satb])
            bO = nextbank()
            for h in range(8):
                p, hh = h // 2, h % 2
                hc = slice(h * 64, (h + 1) * 64)
                mm(ps[:, bO, hc], fmpad[:, p, hh, 1, :], stb_old[:, p, hh * 64:(hh + 1) * 64], True, False, [B_fmpad, Bst_old], [B_ps[bO]])
                mm(ps[:, bO, hc], AM[:, h, 128:256], vbf[:, hc], False, False, [B_AM[h], B_vbf], [B_ps[bO]])
                mm(ps[:, bO, hc], AM[:, h, 384:512], satb[:, hc], False, True, [B_AM[h], B_satb], [B_ps[bO]])
            bU = nextbank()
            for p in range(4):
                pc = slice(p * 128, (p + 1) * 128)
                mm(ps[:, bU, pc], khat[:, pc], vbf[:, pc], True, False, [B_khat, B_vbf], [B_ps[bU]])
                mm(ps[:, bU, pc], bhat[:, pc], satb[:, pc], False, True, [B_bhat, B_satb], [B_ps[bU]])
            for p in range(4):
                stt(STf[:, l, p, :], STf[:, l, p, :], WC[:, 2 * p:2 * p + 1], ps[:, bU, p * 128:(p + 1) * 128], ALU.mult, ALU.add,
                    [B_ST[l], B[wcn], B_ps[bU]], [B_ST[l]])
            cp("act", STb[:, l, 1 - par].rearrange("p a n -> p (a n)"), STf[:, l].rearrange("p a n -> p (a n)"), [B_ST[l]], [B_STb[l][1 - par]])
            cp("act", t_kap, ps[:, bO, :], [B_ps[bO]], [Bkap])

        def wkv_s3b(l, t):
            par = t % 2
            db = t % 2
            khat, bhat, vbf = khat2[db], bhat2[db], vbf2[db]
            B_khat, B_bhat, B_vbf = B_khat2[db], B_bhat2[db], B_vbf2[db]
            (t_sw, t_a, t_v, t_kap, t_kp, t_nb, t_e0, t_e1, t_x) = tmp
            (Bsw, Ba, Bv, Bkap, Bkp, Bnb, Be0, Be1, Bx) = B_tmp
            t_g, Bg, t_bv, Bbv = tg2[db], B_tg2[db], tbv2[db], B_tbv2[db]
            wcn = "small_wc" if db == 0 else "small_wc1"
            WC = small[:, 48 + db * 8:48 + db * 8 + 8]
            row = lambda i: rows[:, i, :]
            CO_ = B["consts"]
            o_sb, Bo = t_kap, Bkap
            cen, Bcen = t_kp, Bkp
            sq_, Bsq = t_nb, Bnb
            mean = small[:, 24:32]
            red(mean, o_sb.rearrange("p (h n) -> p h n", h=8), [Bo], [B["small_c"]])
            yield
            ts("dve", mean, mean, 1.0 / 64, None, ALU.mult, ALU.bypass, [B["small_c"]], [B["small_c"]])
            yield
            o3, c3 = o_sb.rearrange("p (h n) -> p h n", h=8), cen.rearrange("p (h n) -> p h n", h=8)
            tt("dve", c3, o3, mean.unsqueeze(2).to_broadcast([128, 8, 64]), ALU.subtract, [Bo, B["small_c"]], [Bcen])
            yield
            tt("pool", sq_, cen, cen, ALU.mult, [Bcen], [Bsq])
            yield
            var = small[:, 32:40]
            red(var, sq_.rearrange("p (h n) -> p h n", h=8), [Bsq], [B["small_d"]])
            yield
            act(var, var, AF.Sqrt, [B["small_d"]], [B["small_d"]], scale=1.0 / 64, bias=GN_EPS)
            yield
            recip(var, var, [B["small_d"]], [B["small_d"]])
            yield
            tt("dve", c3, c3, var.unsqueeze(2).to_broadcast([128, 8, 64]), ALU.mult, [Bcen, B["small_d"]], [Bcen])
            yield
            tt("pool", cen, cen, row(3), ALU.mult, [Bcen, B["rows"]], [Bcen])
            yield
            tt("dve", cen, cen, row(4), ALU.add, [Bcen, B["rows"]], [Bcen])
            yield
            tt("pool", cen, cen, t_bv, ALU.add, [Bcen, Bbv], [Bcen])
            yield
            tt("dve", yatm, cen, t_g, ALU.mult, [Bcen, Bg], [B_yatm])
            yield
            b = nextbank()
            psb = ps[:, b, :].bitcast(BF16)
            for p in range(4):
                tr(psb[:, p * 128:(p + 1) * 128], yatm[:, p * 128:(p + 1) * 128], identb[:], [B_yatm, CO_], [B_ps[b]])
            cp("act", yaT[:, :, t * 128:(t + 1) * 128], psb[:, 0:512].rearrange("p (a n) -> p a n", a=4), [B_ps[b]], [B_yaT])
            yield

        def ffn(l):
            co = l * NCOL
            rmsnorm(co + 8, lambda kc: h2T[:, kc, :], lambda kc: B_h2T)
            fwo, fbo = co + 28, co + 28 + 132
            hr = lambda kc: h2T[:, kc, :]
            cur = None

            def ffn_fin(i_):
                j2 = i_ % 2
                act(sgl[j2], ag[j2], AF.Silu, [B_ag[j2]], [B_sgl[j2]])
                tt("pool", actb[:, i_, :], sgl[j2], au[j2], ALU.mult, [B_sgl[j2], B_au[j2]], [B_act[i_]])

            for i in range(NFC):
                i2 = i % 2
                if i % 2 == 0:
                    cur = acquire(20 + i // 2)
                c4 = cur[0].rearrange("p (c k n) -> p c k n", c=4, k=8)
                bg = proj_fm(c4[:, i2], cur[1], hr, [B_h2T])
                bu = proj_fm(c4[:, 2 + i2], cur[1], hr, [B_h2T])
                for (bank, z, Bz, a_, Ba_, fc) in ((bg, zg[i2], B_zg[i2], ag[i2], B_ag[i2], i), (bu, zu[i2], B_zu[i2], au[i2], B_au[i2], NFC + i)):
                    cp("pool", z[:, 0:2], ztail[:, l, fc, :], [B["ztail"]], [Bz])
                    cp("act", z[:, 2:514], ps[:, bank, :], [B_ps[bank]], [Bz])
                    cp("pool", ztail[:, l, fc, :], z[:, 512:514], [Bz], [B["ztail"]])
                    w = lambda k_: cols[:, fwo + fc * 3 + k_:fwo + fc * 3 + k_ + 1]
                    if fc < NFC:
                        act(a_, z[:, 2:514], AF.Identity, [Bz, B["cols"]], [Ba_], scale=w(2), bias=cols[:, fbo + fc:fbo + fc + 1])
                    else:
                        ts("dve", a_, z[:, 2:514], w(2), cols[:, fbo + fc:fbo + fc + 1], ALU.mult, ALU.add, [Bz, B["cols"]], [Ba_])
                    stt(a_, z[:, 1:513], w(1), a_, ALU.mult, ALU.add, [Bz, Ba_, B["cols"]], [Ba_])
                    stt(a_, z[:, 0:512], w(0), a_, ALU.mult, ALU.add, [Bz, Ba_, B["cols"]], [Ba_])
                if i >= 1:
                    ffn_fin(i - 1)
            ffn_fin(NFC - 1)
            for j in range(8):
                wd_, Bwd = acquire(31 + j)
                wd3 = wd_[:, 0:NFC * 128].rearrange("p (k n) -> p k n", k=NFC)
                b = nextbank()
                for fc in range(NFC):
                    mm(ps[:, b, :], wd3[:, fc, :], actb[:, fc, :], fc == 0, fc == NFC - 1, [Bwd, B_act[fc]], [B_ps[b]])
                tt("dve", xT[:, j, :], xT[:, j, :], ps[:, b, :], ALU.add, [B_xT[j], B_ps[b]], [B_xT[j]])

        for s in range(n_seq):
            for g in range(n_groups):
                r0 = (s * n_groups + g) * TG
                if g == 0:
                    T.new_epoch()
                if g == 0:
                    mset("pool", STf[:], 0.0, B_ST)
                    mset("pool", STb[:], 0.0, [B_STb[0][0], B_STb[0][1], B_STb[1][0], B_STb[1][1]])
                    mset("pool", hprev[:], 0.0, [B["hprev"]])
                    mset("pool", cutail[:], 0.0, [B["cutail"]])
                    mset("pool", ztail[:], 0.0, [B["ztail"]])
                if s == 0 and g == 0:
                    T.dma(xin, x_d[r0:r0 + TG, :].rearrange("(t p) d -> p t d", p=128), ld_sem, [], [B_xin])
                for kc in range(8):
                    b = nextbank()
                    for t in range(4):
                        tr(ps[:, b, t * 128:(t + 1) * 128], xin[:, t, kc * 128:(kc + 1) * 128], identf[:], [B_xin, B["consts"]], [B_ps[b]])
                    cp(av(), xT[:, kc, :], ps[:, b, :], [B_ps[b]], [B_xT[kc]])
                for l in range(n_layers):
                    mixer(l, g == 0)
                    ffn(l)
                if not (s == n_seq - 1 and g == n_groups - 1):
                    r1 = r0 + TG
                    T.dma(xin, x_d[r1:r1 + TG, :].rearrange("(t p) d -> p t d", p=128), ld_sem, [], [B_xin])
                rmsnorm(2 * NCOL, lambda kc: yTf[:, kc, :], lambda kc: B_yT[kc])
                for t in range(4):
                    for hf in range(2):
                        b = nextbank()
                        for kq in range(4):
                            kc = hf * 4 + kq
                            tr(ps[:, b, kq * 128:(kq + 1) * 128], yTf[:, kc, t * 128:(t + 1) * 128], identf[:], [B_yT[kc], B["consts"]], [B_ps[b]])
                        cp(av(), xtm[:, t, hf * 512:(hf + 1) * 512], ps[:, b, :], [B_ps[b]], [B_xtm])
                T.dma(y_d[r0:r0 + TG, :].rearrange("(t p) d -> p t d", p=128), xtm, st_sem, [B_xtm], [])
        for s_ in T.dsems:
            if s_.count:
                nc.sync.wait_ge(s_.h, s_.count)
        print("[kernel] instructions emitted:", T.ninst, {k: v.count for k, v in T.esem.items()})
    return nc


def _km(W):
    return np.ascontiguousarray(W.reshape(-1, 128, W.shape[1]).transpose(1, 0, 2))


def _pad_tile(a):
    a = a.reshape(128, -1)
    out = np.zeros((128, SLOT), np.float32)
    out[:, :a.shape[1]] = a
    return out


def host_layout(inp):
    f = lambda k: np.asarray(inp[k], np.float32)
    w_in, proj_a, proj_b, w_out, w_up, w_down = f("w_in"), f("proj_a"), f("proj_b"), f("w_out"), f("w_up"), f("w_down")
    wsrc = np.zeros((2 * SPL, 128, SLOT), np.float32)
    for l in range(2):
        t = []
        t.append(_pad_tile(_km(w_in[l][:, 1536:1792])))
        for q in range(3):
            t.append(_pad_tile(_km(w_in[l][:, q * 512:(q + 1) * 512])))
        chunks = []
        for j in range(4):
            chunks += [1792 + j * 128, 2816 + j * 128, 2304 + j * 128]
        for i in range(3):
            t.append(_pad_tile(np.stack([_km(w_in[l][:, c:c + 128]) for c in chunks[i * 4:(i + 1) * 4]], axis=1)))
        for j in range(8):
            t.append(_pad_tile(np.concatenate([_km(w_in[l][:, 3328 + j * 128:3328 + (j + 1) * 128]),
                                               _km(w_in[l][:, 4352 + j * 128:4352 + (j + 1) * 128]),
                                               _km(proj_a[l][:, j * 128:(j + 1) * 128]),
                                               _km(proj_b[l][:, j * 128:(j + 1) * 128])], axis=1)))
        for i in range(2):
            t.append(_pad_tile(np.stack([_km(w_out[l][:, c * 128:(c + 1) * 128]) for c in range(i * 4, i * 4 + 4)], axis=1)))
        for i in range(11):
            cc = [2 * i * 128, (2 * i + 1) * 128, DFF + 2 * i * 128, DFF + (2 * i + 1) * 128]
            t.append(_pad_tile(np.stack([_km(w_up[l][:, c:c + 128]) for c in cc], axis=1)))
        for j in range(8):
            t.append(_pad_tile(_km(w_down[l][:, j * 128:(j + 1) * 128])))
        assert len(t) == SPL
        wsrc[l * SPL:(l + 1) * SPL] = np.stack(t)
    rows = np.zeros((2, 128, NROW), np.float32)
    cols = np.zeros((128, 2 * NCOL + 8), np.float32)
    lup = np.zeros((2, 128, NLUP), np.float32)
    for l in range(2):
        vecs = [f("k_k")[l], f("k_a")[l], f("r_k")[l].reshape(-1), f("ln_x_w")[l], f("ln_x_b")[l], f("mu_shift")[l]]
        rows[l] = np.broadcast_to(np.concatenate(vecs)[None, :], (128, NROW))
        c = l * NCOL
        cols[:, c:c + 8] = f("norm_mix_g")[l].reshape(8, 128).T
        cols[:, c + 8:c + 16] = f("norm_ffn_g")[l].reshape(8, 128).T
        cols[:, c + 16:c + 28] = f("conv_w")[l].reshape(3, 4, 128).transpose(2, 1, 0).reshape(128, 12)
        cols[:, c + 28:c + 160] = f("ffn_conv_w")[l].reshape(3, 44, 128).transpose(2, 1, 0).reshape(128, 132)
        cols[:, c + 160:c + 204] = f("ffn_conv_b")[l].reshape(44, 128).T
        lup[l, 0:64, 0:512] = f("decay_up")[l]
        lup[l, 64, 0:512] = f("w0")[l]
        lup[l, 0:64, 512:1024] = f("a_up")[l]
        lup[l, 64, 512:1024] = f("a0")[l]
        lup[l, :, 1024:1536] = f("g_up")[l]
        if l == 1:
            lup[l, 0:32, 1536:2048] = f("vres_up")[0]
            lup[l, 32, 1536:2048] = f("v0")[0]
            lup[l, :, 2048:2304] = _km(f("vres_down")[0]).reshape(128, 256)
    cols[:, 2 * NCOL:2 * NCOL + 8] = f("norm_final_g").reshape(8, 128).T
    brows = np.zeros((128, 2, 512), np.float32)
    for r, vec in enumerate([f("w0")[0], f("a0")[0], f("w0")[1], f("a0")[1], f("v0")[0]]):
        brows[32 * (r % 3), r // 3] = vec
    s_, t_ = np.arange(128)[:, None], np.arange(128)[None, :]
    cst = np.concatenate([np.eye(128), (s_ <= t_), (s_ < t_), (s_ > t_)], axis=1).astype(np.float32)
    return dict(wsrc=wsrc, rows=rows, cols=cols, lup=lup, cst=cst, brows=brows.reshape(128, 1024))


_NC_CACHE = {}


N_LAUNCH = 1


def kernel(**inputs):
    x = np.asarray(inputs["x"], np.float32)
    bsz = x.shape[0]
    per = bsz // N_CORES
    pl = per // N_LAUNCH
    shared = host_layout(inputs)
    key = (pl, SEQ // TG)
    if key not in _NC_CACHE:
        _NC_CACHE[key] = build_program(n_seq=pl, n_groups=SEQ // TG)
    nc = _NC_CACHE[key]
    out = np.zeros((bsz, SEQ, D), np.float32)
    for h in range(N_LAUNCH):
        in_maps = []
        for c in range(N_CORES):
            m = dict(shared)
            b0 = c * per + h * pl
            m["x"] = np.ascontiguousarray(x[b0:b0 + pl].reshape(pl * SEQ, D))
            in_maps.append(m)
        res = run_bass_kernel_spmd(nc, in_maps, core_ids=list(range(N_CORES)))
        for c in range(N_CORES):
            b0 = c * per + h * pl
            out[b0:b0 + pl] = np.asarray(res.results[c]["y"], np.float32).reshape(pl, SEQ, D)
    return out
```

```python
import math
import itertools
from contextlib import ExitStack

import numpy as np
import concourse.bass as bass
import concourse.mybir as mybir
from concourse.bass_utils import run_bass_kernel_spmd

F32 = mybir.dt.float32
BF16 = mybir.dt.bfloat16
AF = mybir.ActivationFunctionType
ALU = mybir.AluOpType
AX = mybir.AxisListType

N_CORES = 8
D = 1024
SEQ = 2048
TG = 512
NCH = TG // 128
DFF = 2816
NFC = DFF // 128
C0 = math.exp(-0.5)
RMS_EPS = 1e-6
GN_EPS = 64 * 1e-5
SLOT = 4096
NSLOT = 4
TPL = 39
SPL = 36
NROW = 5 * 512 + 1792
NCOL = 8 + 8 + 12 + 132 + 44
NLUP = 4 * 512 + 256 + 3 * 512
ARENA = 123008


class Sem:
    def __init__(self, h, owner):
        self.h, self.owner, self.count = h, owner, 0


class Buf:
    __slots__ = ("region", "lo", "hi", "writers", "readers", "name")

    def __init__(self, region, lo, hi, name=""):
        self.region, self.lo, self.hi, self.name = region, lo, hi, name
        self.writers, self.readers = {}, {}


class Trk:
    def __init__(self, nc, es):
        self.nc = nc
        self.eng = {"pe": nc.tensor, "act": nc.scalar, "dve": nc.vector, "pool": nc.gpsimd, "sp": nc.sync}
        self.esem = {k: Sem(es.enter_context(nc.semaphore("s_" + k)), k) for k in ("pe", "act", "dve", "pool")}
        self.es = es
        self.waited = {}
        self.regions = {}
        self.dsems = []
        self.ninst = 0

    def new_epoch(self):
        self.epoch = getattr(self, "epoch", 0) + 1
        for k in ("pe", "act", "dve", "pool"):
            self.esem[k] = Sem(self.es.enter_context(self.nc.semaphore("s_%s_%d" % (k, self.epoch))), k)

    def dsem(self, name):
        s = Sem(self.es.enter_context(self.nc.semaphore(name)), None)
        self.dsems.append(s)
        return s

    def buf(self, region, lo=0, hi=1 << 30, name=""):
        b = Buf(region, lo, hi, name)
        self.regions.setdefault(region, []).append(b)
        return b

    def _deps(self, engname, reads, writes):
        deps = {}

        def add(sem, val, raw):
            if sem.owner == engname and engname == "pe":
                return
            if deps.get(sem, 0) < val:
                deps[sem] = val

        for b in reads:
            for o in self.regions[b.region]:
                if o.lo < b.hi and b.lo < o.hi:
                    for s, v in o.writers.items():
                        add(s, v, True)
        for b in writes:
            for o in self.regions[b.region]:
                if o.lo < b.hi and b.lo < o.hi:
                    for s, v in o.writers.items():
                        add(s, v, False)
                    for s, v in o.readers.items():
                        add(s, v, False)
        e = self.eng[engname]
        for s, v in deps.items():
            key = (engname, s)
            if self.waited.get(key, 0) < v:
                e.wait_ge(s.h, v)
                self.waited[key] = v

    def _record(self, sem, val, reads, writes):
        for b in reads:
            b.readers[sem] = val
        for b in writes:
            b.readers = {}
            b.writers = {sem: val}

    def emit(self, engname, fn, reads, writes):
        self._deps(engname, reads, writes)
        inst = fn()
        s = self.esem[engname]
        s.count += 1
        inst.then_inc(s.h, 1)
        self._record(s, s.count, reads, writes)
        self.ninst += 1

    def dma(self, out, in_, sem, reads, writes):
        self._deps("sp", reads, writes)
        inst = self.nc.sync.dma_start(out=out, in_=in_)
        sem.count += 16
        inst.then_inc(sem.h, 16)
        self._record(sem, sem.count, reads, writes)
        self.ninst += 1


def build_program(n_seq=4, n_groups=SEQ // TG, n_layers=2):
    nc = bass.Bass("TRN2", target_bir_lowering=False)
    ntok = n_seq * n_groups * TG
    x_d = nc.dram_tensor("x", [ntok, D], F32, kind="ExternalInput").ap()
    wsrc_d = nc.dram_tensor("wsrc", [2 * SPL, 128, SLOT], F32, kind="ExternalInput").ap()
    rows_d = nc.dram_tensor("rows", [2, 128, NROW], F32, kind="ExternalInput").ap()
    cols_d = nc.dram_tensor("cols", [128, 2 * NCOL + 8], F32, kind="ExternalInput").ap()
    lup_d = nc.dram_tensor("lup", [2, 128, NLUP], F32, kind="ExternalInput").ap()
    cst_d = nc.dram_tensor("cst", [128, 4 * 128], F32, kind="ExternalInput").ap()
    brows_d = nc.dram_tensor("brows", [128, 1024], F32, kind="ExternalInput").ap()
    y_d = nc.dram_tensor("y", [ntok, D], F32, kind="ExternalOutput").ap()
    wbf_d = nc.dram_tensor("wbf", [2 * TPL, 128, SLOT], BF16, kind="Internal").ap()

    with ExitStack() as es:
        T = Trk(nc, es)

        def sb(name, shape, dt):
            return es.enter_context(nc.sbuf_tensor("sb_" + name, shape, dt))

        identf = sb("identf", [128, 128], F32)
        uiu_f = sb("uiu_f", [128, 128], F32)
        usu_f = sb("usu_f", [128, 128], F32)
        usl_f = sb("usl_f", [128, 128], F32)
        onescol = sb("onescol", [128, 2], F32)
        identb = sb("identb", [128, 128], BF16)
        onesb = sb("onesb", [128, 128], BF16)
        mask4 = sb("mask4", [128, 512], BF16)
        masksl = sb("masksl", [128, 128], BF16)
        rows = sb("rows", [128, 5, 512], F32)
        cols = sb("cols", [128, 2 * NCOL + 8], F32)
        dupx = sb("dupx", [128, 2, 512], BF16)
        aupx = sb("aupx", [128, 2, 512], BF16)
        gupb = sb("gupb", [128, 2, 512], BF16)
        vupx = sb("vupx", [128, 512], BF16)
        vdn = sb("vdn", [128, 8, 32], BF16)
        blo = sb("blo", [128, 2, 512], BF16)
        STf = sb("STf", [128, 2, 4, 128], F32)
        STb = sb("STb", [128, 2, 2, 4, 128], BF16)
        hprev = sb("hprev", [128, 2, 8], BF16)
        cutail = sb("cutail", [128, 2, 4, 2], F32)
        ztail = sb("ztail", [128, 2, 44, 2], F32)
        xT = sb("xT", [128, 8, TG], F32)
        ring = sb("ring", [128, NSLOT, SLOT], BF16)
        rstd = sb("rstd", [128, TG], F32)
        sqk = sb("sqk", [128, 2, TG], BF16)
        small = sb("small", [128, 64], F32)
        arena = sb("arena", [128, ARENA // 4], F32)
        ps = es.enter_context(nc.psum_tensor("ps", [128, 8, 512], F32))

        B = {}
        for nm in ("consts", "rows", "cols", "lora", "hprev", "cutail", "ztail", "rstd", "rsq", "small_wc", "small_wc1",
                   "small_a", "small_b", "small_c", "small_d", "small_e"):
            B[nm] = T.buf(nm)
        B_ST = [T.buf("STf%d" % l) for l in range(2)]
        B_STb = [[T.buf("STb%d_%d" % (l, p)) for p in range(2)] for l in range(2)]
        B_xT = [T.buf("xT%d" % k) for k in range(8)]
        B_sqk = [T.buf("sqk%d" % k) for k in range(2)]
        B_ring = [T.buf("ring%d" % k) for k in range(NSLOT)]
        B_ps = [T.buf("ps%d" % k) for k in range(8)]
        B_wbf = [T.buf("wbf%d" % k) for k in range(2 * TPL)]
        ring_sem = [T.dsem("rs%d" % k) for k in range(NSLOT)]
        ld_sem = T.dsem("ld")
        rows_sem = T.dsem("rowsld")
        st_sem = T.dsem("st")
        pin_sem = [T.dsem("pin%d" % k) for k in range(2)]
        pout_sem = [T.dsem("pout%d" % k) for k in range(4)]
        misc_sem = [T.dsem("misc%d" % k) for k in range(6)]

        def AB(off, nbytes, name=""):
            assert off % 4 == 0 and off + nbytes <= ARENA, (name, off, nbytes)
            return T.buf("arena", off, off + nbytes, name)

        def AV(off, dt, shape):
            n = int(np.prod(shape))
            if dt == F32:
                v = arena[:, off // 4: off // 4 + n]
            else:
                assert n % 2 == 0
                v = arena[:, off // 4: off // 4 + n // 2].bitcast(BF16)
            if len(shape) == 2:
                return v.rearrange("p (a b) -> p a b", a=shape[0])
            if len(shape) == 3:
                return v.rearrange("p (a b c) -> p a b c", a=shape[0], b=shape[1])
            if len(shape) == 4:
                return v.rearrange("p (a b c d) -> p a b c d", a=shape[0], b=shape[1], c=shape[2])
            return v

        o = 0
        hT = AV(o, BF16, [8, 514]); B_hT = AB(o, 8224, "hT"); o += 8224
        tw = AV(o, BF16, [512]); B_tw = AB(o, 1024); o += 1024
        adx = AV(o, BF16, [512]); B_adx = AB(o, 1024); o += 1024
        sgd = AV(o, BF16, [512]); B_sgd = AB(o, 1024); o += 1024
        hvx = AV(o, BF16, [512]); B_hvx = AB(o, 1024); o += 1024
        RAW_OFF = o
        raw = AV(o, F32, [3, NCH, 512]); B_raw = [[AB(o + (q * NCH + t) * 2048, 2048) for t in range(NCH)] for q in range(3)]; o += 3 * NCH * 2048
        yaT = AV(o, BF16, [4, TG]); B_yaT = AB(o, 4096); o += 4096
        ybT = AV(o, BF16, [4, TG]); B_ybT = [AB(o + j * 1024, 1024) for j in range(4)]; o += 4096
        TMP_OFF = o
        NTMP = 9
        tmp = [AV(o + i * 2048, F32, [512]) for i in range(NTMP)]
        B_tmp = [AB(o + i * 2048, 2048) for i in range(NTMP)]
        o += NTMP * 2048
        tmB = AV(o, BF16, [4, 512]); B_tmB = [AB(o + q * 1024, 1024) for q in range(4)]; o += 4096
        khat2 = [AV(o + i * 1024, BF16, [512]) for i in range(2)]; B_khat2 = [AB(o + i * 1024, 1024) for i in range(2)]; o += 2048
        bhat2 = [AV(o + i * 1024, BF16, [512]) for i in range(2)]; B_bhat2 = [AB(o + i * 1024, 1024) for i in range(2)]; o += 2048
        vbf2 = [AV(o + i * 1024, BF16, [512]) for i in range(2)]; B_vbf2 = [AB(o + i * 1024, 1024) for i in range(2)]; o += 2048
        tg2 = [AV(o + i * 2048, F32, [512]) for i in range(2)]; B_tg2 = [AB(o + i * 2048, 2048) for i in range(2)]; o += 4096
        tbv2 = [AV(o + i * 2048, F32, [512]) for i in range(2)]; B_tbv2 = [AB(o + i * 2048, 2048) for i in range(2)]; o += 4096
        FMPAD_OFF = o
        fmpad = AV(o, BF16, [4, 2, 2, 128]); B_fmpad = AB(o, 4096); o += 4096
        fmkb = AV(o, BF16, [4, 2, 128]); B_fmkb = AB(o, 2048); o += 2048
        AM_OFF = o
        AM = AV(o, BF16, [8, 512]); B_AM = [AB(o + h * 1024, 1024) for h in range(8)]; o += 8192
        PT = [AV(o + i * 2048, BF16, [8, 128]) for i in range(2)]
        B_PT = [[AB(o + i * 2048 + hf * 1024, 1024) for hf in range(2)] for i in range(2)]; o += 4096
        PY = [AV(o + i * 4096, BF16, [8, 2, 128]) for i in range(2)]
        B_PY = [[AB(o + i * 4096 + hf * 2048, 2048) for hf in range(2)] for i in range(2)]; o += 8192
        xtb = AV(o, BF16, [512]); B_xtb = AB(o, 1024); o += 1024
        satb = AV(o, BF16, [512]); B_satb = AB(o, 1024); o += 1024
        yatm = AV(o, BF16, [512]); B_yatm = AB(o, 1024); o += 1024
        VF_OFF = o
        vfirst = AV(o, F32, [NCH, 512]); B_vf = [AB(o + t * 2048, 2048) for t in range(NCH)]; o += NCH * 2048
        assert o <= ARENA, o
        o = RAW_OFF
        sga = [AV(o + i * 1024, BF16, [512]) for i in range(2)]; B_sga = [AB(o + i * 1024, 1024) for i in range(2)]; o += 2048
        sgb = [AV(o + i * 1024, BF16, [512]) for i in range(2)]; B_sgb = [AB(o + i * 1024, 1024) for i in range(2)]; o += 2048
        m1 = [AV(o + i * 2048, F32, [512]) for i in range(2)]; B_m1 = [AB(o + i * 2048, 2048) for i in range(2)]; o += 4096
        m2 = [AV(o + i * 2048, F32, [512]) for i in range(2)]; B_m2 = [AB(o + i * 2048, 2048) for i in range(2)]; o += 4096
        merged = AV(o, BF16, [8, TG]); B_merged = [AB(o + j * 1024, 1024) for j in range(8)]; o += 8192
        assert o <= RAW_OFF + 3 * NCH * 2048
        o = AM_OFF
        usb = AV(o, F32, [512]); B_usb = AB(o, 2048); o += 2048
        cu = AV(o, F32, [514]); B_cu = AB(o, 2064); o += 2064
        cacc = AV(o, F32, [512]); B_cacc = AB(o, 2048); o += 2048
        o = 0
        h2T = AV(o, BF16, [8, TG]); B_h2T = AB(o, 8192); o += 8192
        actb = AV(o, BF16, [NFC, TG]); B_act = [AB(o + i * 1024, 1024) for i in range(NFC)]; o += NFC * 1024
        zg = [AV(o + i * 2064, F32, [514]) for i in range(2)]; B_zg = [AB(o + i * 2064, 2064) for i in range(2)]; o += 4128
        zu = [AV(o + i * 2064, F32, [514]) for i in range(2)]; B_zu = [AB(o + i * 2064, 2064) for i in range(2)]; o += 4128
        ag = [AV(o + i * 2048, F32, [512]) for i in range(2)]; B_ag = [AB(o + i * 2048, 2048) for i in range(2)]; o += 4096
        au = [AV(o + i * 2048, F32, [512]) for i in range(2)]; B_au = [AB(o + i * 2048, 2048) for i in range(2)]; o += 4096
        sgl = [AV(o + i * 2048, F32, [512]) for i in range(2)]; B_sgl = [AB(o + i * 2048, 2048) for i in range(2)]; o += 4096
        assert o <= VF_OFF
        xtm = AV(RAW_OFF, F32, [4, 1024]); B_xtm = AB(RAW_OFF, 16384, "xtm")
        xin = AV(RAW_OFF + 16384, F32, [4, 1024]); B_xin = AB(RAW_OFF + 16384, 16384, "xin")
        assert RAW_OFF + 32768 <= TMP_OFF
        yTf = AV(TMP_OFF, F32, [8, TG]); B_yT = [AB(TMP_OFF + k * 2048, 2048) for k in range(8)]
        pin = [AV(i * 16384, F32, [SLOT]) for i in range(2)]; B_pin = [AB(i * 16384, 16384) for i in range(2)]
        pout = [AV(32768 + i * 8192, BF16, [SLOT]) for i in range(4)]; B_pout = [AB(32768 + i * 8192, 8192) for i in range(4)]
        murow = AV(65536, F32, [1792]); B_mu = AB(65536, 7168)
        omrow = AV(65536 + 7168, F32, [1792]); B_om = AB(65536 + 7168, 7168)
        lupst = AV(81920, F32, [NLUP]); B_lupst = AB(81920, NLUP * 4)
        lotmp = AV(81920 + NLUP * 4, F32, [512]); B_lotmp = AB(81920 + NLUP * 4, 2048)
        cstst = AV(100352, F32, [512]); B_cstst = AB(100352, 2048)

        def mm(out, lhsT, rhs, start, stop, R, W):
            T.emit("pe", lambda: nc.tensor.matmul(out, lhsT=lhsT, rhs=rhs, start=start, stop=stop), R, W)

        def tr(out, in_, ident, R, W):
            T.emit("pe", lambda: nc.tensor.transpose(out=out, in_=in_, identity=ident), R, W)

        def act(out, in_, func, R, W, scale=1.0, bias=0.0):
            T.emit("act", lambda: nc.scalar.activation(out=out, in_=in_, func=func, scale=scale, bias=bias), R, W)

        def cp(eng, out, in_, R, W):
            if eng == "act":
                T.emit("act", lambda: nc.scalar.copy(out=out, in_=in_), R, W)
            else:
                T.emit(eng, lambda: T.eng[eng].tensor_copy(out=out, in_=in_), R, W)

        def tt(eng, out, in0, in1, op, R, W):
            T.emit(eng, lambda: T.eng[eng].tensor_tensor(out=out, in0=in0, in1=in1, op=op), R, W)

        def ts(eng, out, in0, s1, s2, op0, op1, R, W):
            if s2 is None:
                T.emit(eng, lambda: T.eng[eng].tensor_scalar(out=out, in0=in0, scalar1=s1, scalar2=None, op0=op0), R, W)
            else:
                T.emit(eng, lambda: T.eng[eng].tensor_scalar(out=out, in0=in0, scalar1=s1, scalar2=s2, op0=op0, op1=op1), R, W)

        def stt(out, in0, scalar, in1, op0, op1, R, W):
            T.emit("dve", lambda: nc.vector.scalar_tensor_tensor(out=out, in0=in0, scalar=scalar, in1=in1, op0=op0, op1=op1), R, W)

        def red(out, in_, R, W):
            T.emit("dve", lambda: nc.vector.tensor_reduce(out=out, in_=in_, axis=AX.X, op=ALU.add), R, W)

        def recip(out, in_, R, W):
            T.emit("dve", lambda: nc.vector.reciprocal(out=out, in_=in_), R, W)

        def mset(eng, ap, val, W):
            T.emit(eng, lambda: T.eng[eng].memset(ap, val), [], W)

        pbank = [0]

        def nextbank():
            b = pbank[0]
            pbank[0] = (b + 1) % 6
            return b + 2

        pbank1 = [0]

        def nextbank_s1():
            pbank1[0] ^= 1
            return pbank1[0]

        rr = [0]

        def vp():
            rr[0] += 1
            return "pool" if rr[0] % 3 == 0 else "dve"

        ca = [0]

        def av():
            ca[0] += 1
            return "act" if ca[0] % 2 == 0 else "dve"

        T.dma(cstst, cst_d, misc_sem[0], [], [B_cstst])
        cp("dve", identf[:], cstst[:, 0:128], [B_cstst], [B["consts"]])
        cp("dve", uiu_f[:], cstst[:, 128:256], [B_cstst], [B["consts"]])
        cp("dve", usu_f[:], cstst[:, 256:384], [B_cstst], [B["consts"]])
        cp("dve", usl_f[:], cstst[:, 384:512], [B_cstst], [B["consts"]])
        cp("dve", identb[:], cstst[:, 0:128], [B_cstst], [B["consts"]])
        cp("dve", masksl[:], cstst[:, 384:512], [B_cstst], [B["consts"]])
        for q in range(4):
            src = cstst[:, 256:384] if q % 2 == 0 else cstst[:, 128:256]
            cp("dve", mask4[:, q * 128:(q + 1) * 128], src, [B_cstst], [B["consts"]])
        mset("pool", onesb[:], 1.0, [B["consts"]])
        mset("pool", onescol[:], 1.0, [B["consts"]])
        T.dma(cols[:], cols_d, misc_sem[1], [], [B["cols"]])
        T.dma(lupst[:, 0:1024], brows_d, misc_sem[4], [], [B_lupst])
        cp("dve", blo[:].rearrange("p a n -> p (a n)"), lupst[:, 0:1024], [B_lupst], [B["lora"]])
        cp("dve", lupst[:, 1024:2048], blo[:].rearrange("p a n -> p (a n)"), [B["lora"]], [B_lupst])
        tt("dve", lupst[:, 1024:2048], lupst[:, 0:1024], lupst[:, 1024:2048], ALU.subtract, [B_lupst], [B_lupst])
        cp("dve", blo[:].rearrange("p a n -> p (a n)"), lupst[:, 1024:2048], [B_lupst], [B["lora"]])
        for l in range(2):
            T.dma(lupst, lup_d[l], misc_sem[3], [B_lupst], [B_lupst])
            R_, W_ = [B_lupst], [B["lora"]]
            cp("dve", dupx[:, l, :], lupst[:, 0:512], R_, W_)
            cp("dve", aupx[:, l, :], lupst[:, 512:1024], R_, W_)
            cp("dve", gupb[:, l, :], lupst[:, 1024:1536], R_, W_)
            if l == 1:
                cp("dve", vupx[:], lupst[:, 1536:2048], R_, W_)
                cp("dve", vdn[:].rearrange("p a b -> p (a b)"), lupst[:, 2048:2304], R_, W_)
        def src_to_stream(l, s):
            if s == 0:
                return [(0, "lora")]
            if 1 <= s <= 3:
                return [(1 + 2 * (s - 1), "a"), (2 + 2 * (s - 1), "b")]
            if 4 <= s <= 6:
                return [(7 + s - 4, "c")]
            if 7 <= s <= 14:
                return [(10 + s - 7, "c")]
            if 15 <= s <= 16:
                return [(18 + s - 15, "c")]
            if 17 <= s <= 27:
                return [(20 + s - 17, "c")]
            return [(31 + s - 28, "c")]

        pi, po = 0, 0
        for l in range(n_layers):
            T.dma(murow, rows_d[l, :, 2560:2560 + 1792], misc_sem[5], [B_mu, B_om], [B_mu])
            ts("dve", omrow, murow, -1.0, 1.0, ALU.mult, ALU.add, [B_mu], [B_om])
            for s in range(SPL):
                bi_ = pi % 2
                pi += 1
                T.dma(pin[bi_], wsrc_d[l * SPL + s], pin_sem[bi_], [], [B_pin[bi_]])
                for (st, kind) in src_to_stream(l, s):
                    bo = po % 4
                    po += 1
                    eng = ("dve", "act", "pool")[po % 3]
                    if kind == "c":
                        cp(eng, pout[bo], pin[bi_], [B_pin[bi_]], [B_pout[bo]])
                    elif kind == "lora":
                        w3 = pin[bi_][:, 0:2048].rearrange("p (k n) -> p k n", k=8)
                        mu_b = murow[:, 1536:1792].unsqueeze(1).to_broadcast([128, 8, 256])
                        om_b = omrow[:, 1536:1792].unsqueeze(1).to_broadcast([128, 8, 256])
                        tt("dve", pout[bo][:, 0:2048].rearrange("p (k n) -> p k n", k=8), w3, om_b, ALU.mult,
                           [B_pin[bi_], B_om], [B_pout[bo]])
                        tt("dve", pout[bo][:, 2048:4096].rearrange("p (k n) -> p k n", k=8), w3, mu_b, ALU.mult,
                           [B_pin[bi_], B_mu], [B_pout[bo]])
                    else:
                        q = (st - 1) // 2
                        w3 = pin[bi_].rearrange("p (k n) -> p k n", k=8)
                        rowsrc = omrow if kind == "a" else murow
                        m_b = rowsrc[:, q * 512:(q + 1) * 512].unsqueeze(1).to_broadcast([128, 8, 512])
                        tt("dve" if kind == "a" else "pool", pout[bo].rearrange("p (k n) -> p k n", k=8), w3, m_b, ALU.mult,
                           [B_pin[bi_], B_mu, B_om], [B_pout[bo]])
                    T.dma(wbf_d[l * TPL + st], pout[bo], pout_sem[bo], [B_pout[bo]], [B_wbf[l * TPL + st]])

        for s_ in T.dsems:
            if s_.count:
                nc.sync.wait_ge(s_.h, s_.count)
                T.waited[("sp", s_)] = s_.count
        for b_ in B_wbf:
            b_.writers = {}
        mset("pool", arena[:, FMPAD_OFF // 4:FMPAD_OFF // 4 + 1024], 0.0, [B_fmpad])

        order = []
        for g_ in range(n_seq * n_groups):
            for l in range(n_layers):
                order += [l * TPL + k for k in range(TPL)]
        sstate = {"next_load": 0, "next_use": 0}

        def _load(i):
            slot = i % NSLOT
            T.dma(ring[:, slot, :], wbf_d[order[i]], ring_sem[slot], [B_wbf[order[i]]], [B_ring[slot]])

        def acquire(expect):
            i = sstate["next_use"]
            assert order[i] % TPL == expect, (order[i] % TPL, expect)
            if i == 0:
                for j in range(min(NSLOT - 1, len(order))):
                    _load(j)
                sstate["next_load"] = min(NSLOT - 1, len(order))
            nl = sstate["next_load"]
            if nl < len(order) and nl <= i + NSLOT - 1 and i >= 1:
                _load(nl)
                sstate["next_load"] = nl + 1
            sstate["next_use"] = i + 1
            slot = i % NSLOT
            return ring[:, slot, :], B_ring[slot]

        class ChunkStream:
            def __init__(self, first_tile):
                self.t, self.c, self.cur = first_tile, 0, None

            def next(self):
                if self.c % 4 == 0:
                    self.cur = acquire(self.t)
                    self.t += 1
                v = self.cur[0].rearrange("p (c k n) -> p c k n", c=4, k=8)[:, self.c % 4]
                self.c += 1
                return v, self.cur[1]

        def rmsnorm(gcol, dst_fn, B_dst_fn):
            bss = nextbank()
            for kc in range(8):
                k2 = kc % 2
                act(sqk[:, k2, :], xT[:, kc, :], AF.Square, [B_xT[kc]], [B_sqk[k2]])
                mm(ps[:, bss, :], onesb[:], sqk[:, k2, :], kc == 0, kc == 7, [B_sqk[k2], B["consts"]], [B_ps[bss]])
            act(rstd[:], ps[:, bss, :], AF.Sqrt, [B_ps[bss]], [B["rstd"]], scale=1.0 / D, bias=RMS_EPS)
            recip(rstd[:], rstd[:], [B["rstd"]], [B["rstd"]])
            for kc in range(8):
                stt(dst_fn(kc), xT[:, kc, :], cols[:, gcol + kc:gcol + kc + 1], rstd[:], ALU.mult, ALU.mult,
                    [B_xT[kc], B["cols"], B["rstd"]], [B_dst_fn(kc)])

        def proj_fm(chunk, Bch, rhs_fn, R_rhs, nk=8):
            b = nextbank()
            for kc in range(nk):
                mm(ps[:, b, :], chunk[:, kc, :], rhs_fn(kc), kc == 0, kc == nk - 1, [Bch] + R_rhs, [B_ps[b]])
            return b

        def mixer(l, first_group):
            co = l * NCOL
            T.dma(rows[:], rows_d[l, :, 0:2560].rearrange("p (a b) -> p a b", a=5), rows_sem, [], [B["rows"]])
            cp("pool", hT[:, :, 1:2], hprev[:, l, :].unsqueeze(2), [B["hprev"]], [B_hT])
            rmsnorm(co, lambda kc: hT[:, kc, 2:514], lambda kc: B_hT)
            cp("pool", hprev[:, l, :].unsqueeze(2), hT[:, :, 513:514], [B_hT], [B["hprev"]])
            ha = lambda kc: hT[:, kc, 2:514]
            hb = lambda kc: hT[:, kc, 1:513]
            mset("pool", tw[64:65, :], 1.0, [B_tw])
            mset("pool", adx[64:65, :], 1.0, [B_adx])
            if l == 1:
                mset("pool", hvx[32:33, :], 1.0, [B_hvx])
            lt, Blt = acquire(0)
            la = lt[:, 0:2048].rearrange("p (k n) -> p k n", k=8)
            lb = lt[:, 2048:4096].rearrange("p (k n) -> p k n", k=8)
            for (c0, c1, dst, Bdst, fn) in ((0, 64, tw, B_tw, AF.Tanh), (64, 128, adx, B_adx, AF.Copy), (128, 256, sgd, B_sgd, AF.Sigmoid)):
                b = nextbank()
                m = c1 - c0
                for kc in range(8):
                    mm(ps[0:m, b, :], la[:, kc, c0:c1], ha(kc), kc == 0, False, [Blt, B_hT], [B_ps[b]])
                    mm(ps[0:m, b, :], lb[:, kc, c0:c1], hb(kc), False, kc == 7, [Blt, B_hT], [B_ps[b]])
                act(dst[0:m, :], ps[0:m, b, :], fn, [B_ps[b]], [Bdst])
            if l == 1:
                b = nextbank()
                for kc in range(8):
                    mm(ps[0:32, b, :], vdn[:, kc, :], ha(kc), kc == 0, kc == 7, [B["lora"], B_hT], [B_ps[b]])
                act(hvx[0:32, :], ps[0:32, b, :], AF.Copy, [B_ps[b]], [B_hvx])
            for q in range(3):
                wa, Bwa = acquire(1 + 2 * q)
                wb_, Bwb = acquire(2 + 2 * q)
                wa3 = wa.rearrange("p (k n) -> p k n", k=8)
                wb3 = wb_.rearrange("p (k n) -> p k n", k=8)
                for t in range(NCH):
                    b = nextbank()
                    for kc in range(8):
                        mm(ps[:, b, :], hT[:, kc, 2 + t * 128:2 + (t + 1) * 128], wa3[:, kc, :], kc == 0, False, [Bwa, B_hT], [B_ps[b]])
                        mm(ps[:, b, :], hT[:, kc, 1 + t * 128:1 + (t + 1) * 128], wb3[:, kc, :], False, kc == 7, [Bwb, B_hT], [B_ps[b]])
                    if q == 2 and l == 0:
                        cp(av(), vfirst[:, t, :], ps[:, b, :], [B_ps[b]], [B_vf[t]])
                    else:
                        cp(av(), raw[:, q, t, :], ps[:, b, :], [B_ps[b]], [B_raw[q][t]])
            cs = ChunkStream(7)
            cwo = co + 16
            g0 = wkv_s1(l, 0)

            def fill0(n):
                for _ in range(n):
                    try:
                        next(g0)
                    except StopIteration:
                        return

            for j in range(4):
                ch, Bch = cs.next()
                b = proj_fm(ch, Bch, ha, [B_hT])
                cp("act", usb, ps[:, b, :], [B_ps[b]], [B_usb])
                fill0(4)
                ch, Bch = cs.next()
                b = proj_fm(ch, Bch, ha, [B_hT])
                cp("pool", cu[:, 0:2], cutail[:, l, j, :], [B["cutail"]], [B_cu])
                tt("dve", cu[:, 2:514], ps[:, b, :], usb, ALU.mult, [B_ps[b], B_usb], [B_cu])
                cp("pool", cutail[:, l, j, :], cu[:, 512:514], [B_cu], [B["cutail"]])
                w = lambda k_: cols[:, cwo + j * 3 + k_:cwo + j * 3 + k_ + 1]
                ts("dve", cacc, cu[:, 0:512], w(0), None, ALU.mult, ALU.bypass, [B_cu, B["cols"]], [B_cacc])
                stt(cacc, cu[:, 1:513], w(1), cacc, ALU.mult, ALU.add, [B_cu, B_cacc, B["cols"]], [B_cacc])
                stt(cacc, cu[:, 2:514], w(2), cacc, ALU.mult, ALU.add, [B_cu, B_cacc, B["cols"]], [B_cacc])
                fill0(5)
                ch, Bch = cs.next()
                b = proj_fm(ch, Bch, ha, [B_hT])
                tt("dve", ybT[:, j, :], ps[:, b, :], cacc, ALU.mult, [B_ps[b], B_cacc], [B_ybT[j]])
                fill0(4)
            for _ in g0:
                pass
            pend = None
            for t in range(NCH):
                gens = [g_ for g_ in (pend, wkv_s1(l, t + 1) if t + 1 < NCH else None) if g_ is not None]
                filler = itertools.chain(*gens)
                wkv_s2(l, t, filler)
                for _ in filler:
                    pass
                wkv_s3a(l, t)
                pend = wkv_s3b(l, t)
            for _ in pend:
                pass
            for j in range(8):
                i2 = j % 2
                mp, Bmp = acquire(10 + j)
                mp3 = mp[:, 0:3072].rearrange("p (k n) -> p k n", k=24)
                b = nextbank()
                for kc in range(8):
                    mm(ps[:, b, :], mp3[:, kc, :], ha(kc), kc == 0, kc == 7, [Bmp, B_hT], [B_ps[b]])
                act(sga[i2], ps[:, b, :], AF.Sigmoid, [B_ps[b]], [B_sga[i2]])
                b = nextbank()
                for kc in range(4):
                    mm(ps[:, b, :], mp3[:, 16 + kc, :], yaT[:, kc, :], kc == 0, kc == 3, [Bmp, B_yaT], [B_ps[b]])
                tt("dve", m1[i2], ps[:, b, :], sga[i2], ALU.mult, [B_ps[b], B_sga[i2]], [B_m1[i2]])
                b = nextbank()
                for kc in range(8):
                    mm(ps[:, b, :], mp3[:, 8 + kc, :], ha(kc), kc == 0, kc == 7, [Bmp, B_hT], [B_ps[b]])
                act(sgb[i2], ps[:, b, :], AF.Sigmoid, [B_ps[b]], [B_sgb[i2]])
                b = nextbank()
                for kc in range(4):
                    mm(ps[:, b, :], mp3[:, 20 + kc, :], ybT[:, kc, :], kc == 0, kc == 3, [Bmp] + B_ybT, [B_ps[b]])
                tt("dve", m2[i2], ps[:, b, :], sgb[i2], ALU.mult, [B_ps[b], B_sgb[i2]], [B_m2[i2]])
                tt("pool", merged[:, j, :], m1[i2], m2[i2], ALU.add, [B_m1[i2], B_m2[i2]], [B_merged[j]])
            cs = ChunkStream(18)
            for j in range(8):
                ch, Bch = cs.next()
                b = proj_fm(ch, Bch, lambda kc: merged[:, kc, :], B_merged)
                tt("dve", xT[:, j, :], xT[:, j, :], ps[:, b, :], ALU.add, [B_xT[j], B_ps[b]], [B_xT[j]])

        def wkv_s1(l, t):
            tok = slice(t * 128, (t + 1) * 128)
            db = t % 2
            khat, bhat, vbf = khat2[db], bhat2[db], vbf2[db]
            B_khat, B_bhat, B_vbf = B_khat2[db], B_bhat2[db], B_vbf2[db]
            r_t, k_t = raw[:, 0, t, :], raw[:, 1, t, :]
            Br, Bk = B_raw[0][t], B_raw[1][t]
            row = lambda i: rows[:, i, :]
            (t_sw, t_a, t_v, t_kap, t_kp, t_nb, t_e0, t_e1, t_x) = tmp
            (Bsw, Ba, Bv, Bkap, Bkp, Bnb, Be0, Be1, Bx) = B_tmp
            t_g, Bg, t_bv, Bbv = tg2[db], B_tg2[db], tbv2[db], B_tbv2[db]
            wcn = "small_wc" if db == 0 else "small_wc1"
            LO, CO_ = B["lora"], B["consts"]
            b = nextbank_s1()
            mm(ps[:, b, :], tw[0:65, tok], dupx[0:65, l, :], True, False, [B_tw, LO], [B_ps[b]])
            r_ = l * 2
            mm(ps[:, b, :], onesb[32 * (r_ % 3):32 * (r_ % 3) + 1, :], blo[32 * (r_ % 3):32 * (r_ % 3) + 1, r_ // 3, :], False, True, [CO_, LO], [B_ps[b]])
            act(t_sw, ps[:, b, :], AF.Sigmoid, [B_ps[b]], [Bsw])
            yield
            b = nextbank_s1()
            mm(ps[:, b, :], adx[0:65, tok], aupx[0:65, l, :], True, False, [B_adx, LO], [B_ps[b]])
            r_ = l * 2 + 1
            mm(ps[:, b, :], onesb[32 * (r_ % 3):32 * (r_ % 3) + 1, :], blo[32 * (r_ % 3):32 * (r_ % 3) + 1, r_ // 3, :], False, True, [CO_, LO], [B_ps[b]])
            act(t_a, ps[:, b, :], AF.Sigmoid, [B_ps[b]], [Ba])
            yield
            b = nextbank_s1()
            mm(ps[:, b, :], sgd[:, tok], gupb[:, l, :], True, True, [B_sgd, LO], [B_ps[b]])
            cp("act", t_g, ps[:, b, :], [B_ps[b]], [Bg])
            yield
            if l == 1:
                b = nextbank_s1()
                mm(ps[:, b, :], hvx[0:33, tok], vupx[0:33, :], True, False, [B_hvx, LO], [B_ps[b]])
                r_ = 4
                mm(ps[:, b, :], onesb[32 * (r_ % 3):32 * (r_ % 3) + 1, :], blo[32 * (r_ % 3):32 * (r_ % 3) + 1, r_ // 3, :], False, True, [CO_, LO], [B_ps[b]])
                act(t_x, ps[:, b, :], AF.Sigmoid, [B_ps[b]], [Bx])
                v_raw = raw[:, 2, t, :]
                tt("dve", t_v, vfirst[:, t, :], v_raw, ALU.subtract, [B_vf[t], B_raw[2][t]], [Bv])
                tt("pool", t_v, t_v, t_x, ALU.mult, [Bv, Bx], [Bv])
                tt("pool", t_v, t_v, v_raw, ALU.add, [Bv, B_raw[2][t]], [Bv])
                v_eff, Bve = t_v, Bv
            else:
                v_eff, Bve = vfirst[:, t, :], B_vf[t]
            WC = small[:, 48 + db * 8:48 + db * 8 + 8]
            tt("dve", t_kap, k_t, row(0), ALU.mult, [Bk, B["rows"]], [Bkap])
            yield
            tt("pool", t_x, t_kap, t_kap, ALU.mult, [Bkap], [Bx])
            yield
            ss = small[:, 8:16]
            red(ss, t_x.rearrange("p (h n) -> p h n", h=8), [Bx], [B["small_a"]])
            yield
            act(ss, ss, AF.Sqrt, [B["small_a"]], [B["small_a"]])
            yield
            ts("dve", ss, ss, 1e-12, None, ALU.max, ALU.bypass, [B["small_a"]], [B["small_a"]])
            yield
            recip(ss, ss, [B["small_a"]], [B["small_a"]])
            yield
            k3 = t_kap.rearrange("p (h n) -> p h n", h=8)
            tt("dve", k3, k3, ss.unsqueeze(2).to_broadcast([128, 8, 64]), ALU.mult, [Bkap, B["small_a"]], [Bkap])
            yield
            stt(t_kp, t_a, -1.0, row(1), ALU.add, ALU.mult, [Ba, B["rows"]], [Bkp])
            yield
            stt(t_kp, t_kp, 1.0, k_t, ALU.add, ALU.mult, [Bkp, Bk], [Bkp])
            yield
            stt(t_nb, t_a, -1.0, t_kap, ALU.mult, ALU.mult, [Ba, Bkap], [Bnb])
            yield
            bce = nextbank_s1()
            mm(ps[:, bce, :], usu_f[:], t_sw, True, True, [CO_, Bsw], [B_ps[bce]])
            act(t_e0, ps[:, bce, :], AF.Exp, [B_ps[bce]], [Be0], scale=-C0)
            yield
            tt("dve", tmB[:, 0, :], t_kap, t_e0, ALU.mult, [Bkap, Be0], [B_tmB[0]])
            bci = nextbank_s1()
            mm(ps[:, bci, :], uiu_f[:], t_sw, True, True, [CO_, Bsw], [B_ps[bci]])
            act(t_e1, ps[:, bci, :], AF.Exp, [B_ps[bci]], [Be1], scale=-C0)
            yield
            tt("pool", tmB[:, 1, :], r_t, t_e1, ALU.mult, [Br, Be1], [B_tmB[1]])
            act(t_e0, ps[:, bci, :], AF.Exp, [B_ps[bci]], [Be0], scale=C0)
            yield
            tt("dve", tmB[:, 2, :], t_kp, t_e0, ALU.mult, [Bkp, Be0], [B_tmB[2]])
            tt("pool", tmB[:, 3, :], t_nb, t_e0, ALU.mult, [Bnb, Be0], [B_tmB[3]])
            yield
            brv = nextbank_s1()
            mm(ps[:, brv, :], usl_f[:], t_sw, True, True, [CO_, Bsw], [B_ps[brv]])
            act(t_e1, ps[:, brv, :], AF.Exp, [B_ps[brv]], [Be1], scale=-C0)
            yield
            tt("dve", khat, t_kp, t_e1, ALU.mult, [Bkp, Be1], [B_khat])
            yield
            tt("pool", bhat, t_nb, t_e1, ALU.mult, [Bnb, Be1], [B_bhat])
            yield
            yield
            btot = nextbank_s1()
            for p in range(4):
                mm(ps[:, btot, 2 * p:2 * p + 2], t_sw[:, p * 128:(p + 1) * 128], onescol[:, 0:2], True, True, [CO_, Bsw], [B_ps[btot]])
            act(WC, ps[:, btot, 0:8], AF.Exp, [B_ps[btot]], [B[wcn]], scale=-C0)
            yield
            yield
            cp("pool", vbf, v_eff, [Bve], [B_vbf])
            yield
            tt("dve", t_x, r_t, t_kp, ALU.mult, [Br, Bkp], [Bx])
            yield
            tt("pool", t_x, t_x, row(2), ALU.mult, [Bx, B["rows"]], [Bx])
            yield
            bs = small[:, 16:24]
            red(bs, t_x.rearrange("p (h n) -> p h n", h=8), [Bx], [B["small_b"]])
            yield
            tt("dve", t_bv.rearrange("p (h n) -> p h n", h=8), v_eff.rearrange("p (h n) -> p h n", h=8),
               bs.unsqueeze(2).to_broadcast([128, 8, 64]), ALU.mult, [Bve, B["small_b"]], [Bbv])

        def wkv_s2(l, t, filler):
            CO_ = B["consts"]

            def fill(n):
                if filler is None:
                    return
                for _ in range(n):
                    try:
                        next(filler)
                    except StopIteration:
                        return
            for half in range(2):
                b = nextbank()
                psb = ps[:, b, :].bitcast(BF16)
                for pp in range(2):
                    p = half * 2 + pp
                    for q in range(4):
                        tr(psb[:, (pp * 4 + q) * 128:(pp * 4 + q + 1) * 128], tmB[:, q, p * 128:(p + 1) * 128], identb[:],
                           [B_tmB[q], CO_], [B_ps[b]])
                v4 = psb.rearrange("p (a q n) -> p a q n", a=2, q=4)
                for hh in range(2):
                    cp("act" if hh == 0 else "dve", fmpad[hh * 64:(hh + 1) * 64, half * 2:half * 2 + 2, hh, :, :],
                       v4[hh * 64:(hh + 1) * 64, :, 0:2, :], [B_ps[b]], [B_fmpad])
                cp("act", fmkb[:, half * 2:half * 2 + 2, :, :], v4[:, :, 2:4, :], [B_ps[b]], [B_fmkb])
                fill(2)
            for h in range(8):
                p, hh = h // 2, h % 2
                b = nextbank()
                rhs = fmpad[:, p, hh, :, :].rearrange("p a n -> p (a n)")
                mm(ps[:, b, 0:256], fmkb[:, p, 0, :], rhs, True, True, [B_fmkb, B_fmpad], [B_ps[b]])
                mm(ps[:, b, 256:512], fmkb[:, p, 1, :], rhs, True, True, [B_fmkb, B_fmpad], [B_ps[b]])
                tt("dve", AM[:, h, :], ps[:, b, :], mask4[:], ALU.mult, [B_ps[b], CO_], [B_AM[h]])
                if h % 2 == 1:
                    fill(1)
            for half in range(2):
                b = nextbank()
                for j in range(4):
                    h = half * 4 + j
                    p, hh = h // 2, h % 2
                    mm(ps[:, b, j * 128:(j + 1) * 128], fmpad[:, p, hh, 0, :], fmkb[:, p, 1, :], True, True, [B_fmkb, B_fmpad], [B_ps[b]])
                tt("dve", PT[0][:, half * 4:half * 4 + 4, :], ps[:, b, :].rearrange("p (a n) -> p a n", a=4),
                   masksl[:].unsqueeze(1).to_broadcast([128, 4, 128]), ALU.mult, [B_ps[b], CO_], [B_PT[0][half]])
                hs = slice(half * 4, half * 4 + 4)
                cp("pool", PY[0][:, hs, 0, :], AM[:, hs, 256:384], B_AM[half * 4:half * 4 + 4], [B_PY[0][half]])
                tt("pool", PY[1][:, hs, 1, :], AM[:, hs, 256:384], identb[:].unsqueeze(1).to_broadcast([128, 4, 128]), ALU.add,
                   B_AM[half * 4:half * 4 + 4] + [CO_], [B_PY[1][half]])
            for kk in range(1, 8):
                cur, nxt = (kk - 1) % 2, kk % 2
                for half in range(2):
                    Bc = [B_PT[cur][half], B_PY[cur][half]]
                    if kk == 1:
                        Bc = Bc
                    bA, bB = nextbank(), nextbank()
                    bC = nextbank() if kk <= 6 else None
                    for j in range(4):
                        h = half * 4 + j
                        bank = bA if j < 2 else bB
                        off = (j % 2) * 256
                        if kk == 1:
                            mm(ps[:, bank, off:off + 128], PT[cur][:, h, :], PY[cur][:, h, 0, :], True, True, Bc, [B_ps[bank]])
                        elif kk <= 6:
                            mm(ps[:, bank, off:off + 256], PT[cur][:, h, :], PY[cur][:, h, :, :].rearrange("p a n -> p (a n)"), True, True, Bc, [B_ps[bank]])
                        else:
                            mm(ps[:, bank, off + 128:off + 256], PT[cur][:, h, :], PY[cur][:, h, 1, :], True, True, Bc, [B_ps[bank]])
                        if kk <= 6:
                            mm(ps[:, bC, j * 128:(j + 1) * 128], PY[cur][:, h, 0, :], PT[cur][:, h, :], True, True, Bc, [B_ps[bC]])
                    for jj, bank in enumerate((bA, bB)):
                        hs = slice(half * 4 + jj * 2, half * 4 + jj * 2 + 2)
                        v3 = ps[:, bank, :].rearrange("p (a q n) -> p a q n", a=2, q=2)
                        if kk <= 6:
                            cp("act", PY[nxt][:, hs, 0, :], v3[:, :, 0, :], [B_ps[bank]], [B_PY[nxt][half]])
                        if kk >= 2:
                            tt("dve", PY[nxt][:, hs, 1, :], v3[:, :, 1, :], PY[cur][:, hs, 1, :], ALU.add, [B_ps[bank], B_PY[cur][half]], [B_PY[nxt][half]])
                    if kk <= 6:
                        cp("act", PT[nxt][:, half * 4:half * 4 + 4, :], ps[:, bC, :].rearrange("p (a n) -> p a n", a=4), [B_ps[bC]], [B_PT[nxt][half]])
                    fill(4)
        def wkv_s3a(l, t):
            par = t % 2
            db = t % 2
            khat, bhat, vbf = khat2[db], bhat2[db], vbf2[db]
            B_khat, B_bhat, B_vbf = B_khat2[db], B_bhat2[db], B_vbf2[db]
            (t_sw, t_a, t_v, t_kap, t_kp, t_nb, t_e0, t_e1, t_x) = tmp
            (Bsw, Ba, Bv, Bkap, Bkp, Bnb, Be0, Be1, Bx) = B_tmp
            t_g, Bg, t_bv, Bbv = tg2[db], B_tg2[db], tbv2[db], B_tbv2[db]
            wcn = "small_wc" if db == 0 else "small_wc1"
            WC = small[:, 48 + db * 8:48 + db * 8 + 8]
            row = lambda i: rows[:, i, :]
            CO_ = B["consts"]
            fin = 7 % 2
            Tm = lambda h: PY[fin][:, h, 1, :]
            BT = B_PY[fin]
            stb_old, Bst_old = STb[:, l, par], B_STb[l][par]
            bX = nextbank()
            for h in range(8):
                p, hh = h // 2, h % 2
                hc = slice(h * 64, (h + 1) * 64)
                mm(ps[:, bX, hc], fmpad[:, p, hh, 0, :], stb_old[:, p, hh * 64:(hh + 1) * 64], True, False, [B_fmpad, Bst_old], [B_ps[bX]])
                mm(ps[:, bX, hc], AM[:, h, 0:128], vbf[:, hc], False, True, [B_AM[h], B_vbf], [B_ps[bX]])
            cp("act", xtb, ps[:, bX, :], [B_ps[bX]], [B_xtb])
            bS = nextbank()
            for h in range(8):
                hc = slice(h * 64, (h + 1) * 64)
                mm(ps[:, bS, hc], Tm(h), xtb[:, hc], True, True, [BT[h // 4], B_xtb], [B_ps[bS]])
            cp("dve", satb, ps[:, bS, :], [B_ps[bS]], [B_satb])
            bO = nextbank()
            for h in range(8):
                p, hh = h // 2, h % 2
                hc = slice(h * 64, (h + 1) * 64)
                mm(ps[:, bO, hc], fmpad[:, p, hh, 1, :], stb_old[:, p, hh * 64:(hh + 1) * 64], True, False, [B_fmpad, Bst_old], [B_ps[bO]])
                mm(ps[:, bO, hc], AM[:, h, 128:256], vbf[:, hc], False, False, [B_AM[h], B_vbf], [B_ps[bO]])
                mm(ps[:, bO, hc], AM[:, h, 384:512], satb[:, hc], False, True, [B_AM[h], B_satb], [B_ps[bO]])
            bU = nextbank()
            for p in range(4):
                pc = slice(p * 128, (p + 1) * 128)
                mm(ps[:, bU, pc], khat[:, pc], vbf[:, pc], True, False, [B_khat, B_vbf], [B_ps[bU]])
                mm(ps[:, bU, pc], bhat[:, pc], satb[:, pc], False, True, [B_bhat, B_satb], [B_ps[bU]])
            for p in range(4):
                stt(STf[:, l, p, :], STf[:, l, p, :], WC[:, 2 * p:2 * p + 1], ps[:, bU, p * 128:(p + 1) * 128], ALU.mult, ALU.add,
                    [B_ST[l], B[wcn], B_ps[bU]], [B_ST[l]])
            cp("act", STb[:, l, 1 - par].rearrange("p a n -> p (a n)"), STf[:, l].rearrange("p a n -> p (a n)"), [B_ST[l]], [B_STb[l][1 - par]])
            cp("act", t_kap, ps[:, bO, :], [B_ps[bO]], [Bkap])

        def wkv_s3b(l, t):
            par = t % 2
            db = t % 2
            khat, bhat, vbf = khat2[db], bhat2[db], vbf2[db]
            B_khat, B_bhat, B_vbf = B_khat2[db], B_bhat2[db], B_vbf2[db]
            (t_sw, t_a, t_v, t_kap, t_kp, t_nb, t_e0, t_e1, t_x) = tmp
            (Bsw, Ba, Bv, Bkap, Bkp, Bnb, Be0, Be1, Bx) = B_tmp
            t_g, Bg, t_bv, Bbv = tg2[db], B_tg2[db], tbv2[db], B_tbv2[db]
            wcn = "small_wc" if db == 0 else "small_wc1"
            WC = small[:, 48 + db * 8:48 + db * 8 + 8]
            row = lambda i: rows[:, i, :]
            CO_ = B["consts"]
            o_sb, Bo = t_kap, Bkap
            cen, Bcen = t_kp, Bkp
            sq_, Bsq = t_nb, Bnb
            mean = small[:, 24:32]
            red(mean, o_sb.rearrange("p (h n) -> p h n", h=8), [Bo], [B["small_c"]])
            yield
            ts("dve", mean, mean, 1.0 / 64, None, ALU.mult, ALU.bypass, [B["small_c"]], [B["small_c"]])
            yield
            o3, c3 = o_sb.rearrange("p (h n) -> p h n", h=8), cen.rearrange("p (h n) -> p h n", h=8)
            tt("dve", c3, o3, mean.unsqueeze(2).to_broadcast([128, 8, 64]), ALU.subtract, [Bo, B["small_c"]], [Bcen])
            yield
            tt("pool", sq_, cen, cen, ALU.mult, [Bcen], [Bsq])
            yield
            var = small[:, 32:40]
            red(var, sq_.rearrange("p (h n) -> p h n", h=8), [Bsq], [B["small_d"]])
            yield
            act(var, var, AF.Sqrt, [B["small_d"]], [B["small_d"]], scale=1.0 / 64, bias=GN_EPS)
            yield
            recip(var, var, [B["small_d"]], [B["small_d"]])
            yield
            tt("dve", c3, c3, var.unsqueeze(2).to_broadcast([128, 8, 64]), ALU.mult, [Bcen, B["small_d"]], [Bcen])
            yield
            tt("pool", cen, cen, row(3), ALU.mult, [Bcen, B["rows"]], [Bcen])
            yield
            tt("dve", cen, cen, row(4), ALU.add, [Bcen, B["rows"]], [Bcen])
            yield
            tt("pool", cen, cen, t_bv, ALU.add, [Bcen, Bbv], [Bcen])
            yield
            tt("dve", yatm, cen, t_g, ALU.mult, [Bcen, Bg], [B_yatm])
            yield
            b = nextbank()
            psb = ps[:, b, :].bitcast(BF16)
            for p in range(4):
                tr(psb[:, p * 128:(p + 1) * 128], yatm[:, p * 128:(p + 1) * 128], identb[:], [B_yatm, CO_], [B_ps[b]])
            cp("act", yaT[:, :, t * 128:(t + 1) * 128], psb[:, 0:512].rearrange("p (a n) -> p a n", a=4), [B_ps[b]], [B_yaT])
            yield

        def ffn(l):
            co = l * NCOL
            rmsnorm(co + 8, lambda kc: h2T[:, kc, :], lambda kc: B_h2T)
            fwo, fbo = co + 28, co + 28 + 132
            hr = lambda kc: h2T[:, kc, :]
            cur = None

            def ffn_fin(i_):
                j2 = i_ % 2
                act(sgl[j2], ag[j2], AF.Silu, [B_ag[j2]], [B_sgl[j2]])
                tt("pool", actb[:, i_, :], sgl[j2], au[j2], ALU.mult, [B_sgl[j2], B_au[j2]], [B_act[i_]])

            for i in range(NFC):
                i2 = i % 2
                if i % 2 == 0:
                    cur = acquire(20 + i // 2)
                c4 = cur[0].rearrange("p (c k n) -> p c k n", c=4, k=8)
                bg = proj_fm(c4[:, i2], cur[1], hr, [B_h2T])
                bu = proj_fm(c4[:, 2 + i2], cur[1], hr, [B_h2T])
                for (bank, z, Bz, a_, Ba_, fc) in ((bg, zg[i2], B_zg[i2], ag[i2], B_ag[i2], i), (bu, zu[i2], B_zu[i2], au[i2], B_au[i2], NFC + i)):
                    cp("pool", z[:, 0:2], ztail[:, l, fc, :], [B["ztail"]], [Bz])
                    cp("act", z[:, 2:514], ps[:, bank, :], [B_ps[bank]], [Bz])
                    cp("pool", ztail[:, l, fc, :], z[:, 512:514], [Bz], [B["ztail"]])
                    w = lambda k_: cols[:, fwo + fc * 3 + k_:fwo + fc * 3 + k_ + 1]
                    if fc < NFC:
                        act(a_, z[:, 2:514], AF.Identity, [Bz, B["cols"]], [Ba_], scale=w(2), bias=cols[:, fbo + fc:fbo + fc + 1])
                    else:
                        ts("dve", a_, z[:, 2:514], w(2), cols[:, fbo + fc:fbo + fc + 1], ALU.mult, ALU.add, [Bz, B["cols"]], [Ba_])
                    stt(a_, z[:, 1:513], w(1), a_, ALU.mult, ALU.add, [Bz, Ba_, B["cols"]], [Ba_])
                    stt(a_, z[:, 0:512], w(0), a_, ALU.mult, ALU.add, [Bz, Ba_, B["cols"]], [Ba_])
                if i >= 1:
                    ffn_fin(i - 1)
            ffn_fin(NFC - 1)
            for j in range(8):
                wd_, Bwd = acquire(31 + j)
                wd3 = wd_[:, 0:NFC * 128].rearrange("p (k n) -> p k n", k=NFC)
                b = nextbank()
                for fc in range(NFC):
                    mm(ps[:, b, :], wd3[:, fc, :], actb[:, fc, :], fc == 0, fc == NFC - 1, [Bwd, B_act[fc]], [B_ps[b]])
                tt("dve", xT[:, j, :], xT[:, j, :], ps[:, b, :], ALU.add, [B_xT[j], B_ps[b]], [B_xT[j]])

        for s in range(n_seq):
            for g in range(n_groups):
                r0 = (s * n_groups + g) * TG
                if g == 0:
                    T.new_epoch()
                if g == 0:
                    mset("pool", STf[:], 0.0, B_ST)
                    mset("pool", STb[:], 0.0, [B_STb[0][0], B_STb[0][1], B_STb[1][0], B_STb[1][1]])
                    mset("pool", hprev[:], 0.0, [B["hprev"]])
                    mset("pool", cutail[:], 0.0, [B["cutail"]])
                    mset("pool", ztail[:], 0.0, [B["ztail"]])
                if s == 0 and g == 0:
                    T.dma(xin, x_d[r0:r0 + TG, :].rearrange("(t p) d -> p t d", p=128), ld_sem, [], [B_xin])
                for kc in range(8):
                    b = nextbank()
                    for t in range(4):
                        tr(ps[:, b, t * 128:(t + 1) * 128], xin[:, t, kc * 128:(kc + 1) * 128], identf[:], [B_xin, B["consts"]], [B_ps[b]])
                    cp(av(), xT[:, kc, :], ps[:, b, :], [B_ps[b]], [B_xT[kc]])
                for l in range(n_layers):
                    mixer(l, g == 0)
                    ffn(l)
                if not (s == n_seq - 1 and g == n_groups - 1):
                    r1 = r0 + TG
                    T.dma(xin, x_d[r1:r1 + TG, :].rearrange("(t p) d -> p t d", p=128), ld_sem, [], [B_xin])
                rmsnorm(2 * NCOL, lambda kc: yTf[:, kc, :], lambda kc: B_yT[kc])
                for t in range(4):
                    for hf in range(2):
                        b = nextbank()
                        for kq in range(4):
                            kc = hf * 4 + kq
                            tr(ps[:, b, kq * 128:(kq + 1) * 128], yTf[:, kc, t * 128:(t + 1) * 128], identf[:], [B_yT[kc], B["consts"]], [B_ps[b]])
                        cp(av(), xtm[:, t, hf * 512:(hf + 1) * 512], ps[:, b, :], [B_ps[b]], [B_xtm])
                T.dma(y_d[r0:r0 + TG, :].rearrange("(t p) d -> p t d", p=128), xtm, st_sem, [B_xtm], [])
        for s_ in T.dsems:
            if s_.count:
                nc.sync.wait_ge(s_.h, s_.count)
        print("[kernel] instructions emitted:", T.ninst, {k: v.count for k, v in T.esem.items()})
    return nc


def _km(W):
    return np.ascontiguousarray(W.reshape(-1, 128, W.shape[1]).transpose(1, 0, 2))


def _pad_tile(a):
    a = a.reshape(128, -1)
    out = np.zeros((128, SLOT), np.float32)
    out[:, :a.shape[1]] = a
    return out


def host_layout(inp):
    f = lambda k: np.asarray(inp[k], np.float32)
    w_in, proj_a, proj_b, w_out, w_up, w_down = f("w_in"), f("proj_a"), f("proj_b"), f("w_out"), f("w_up"), f("w_down")
    wsrc = np.zeros((2 * SPL, 128, SLOT), np.float32)
    for l in range(2):
        t = []
        t.append(_pad_tile(_km(w_in[l][:, 1536:1792])))
        for q in range(3):
            t.append(_pad_tile(_km(w_in[l][:, q * 512:(q + 1) * 512])))
        chunks = []
        for j in range(4):
            chunks += [1792 + j * 128, 2816 + j * 128, 2304 + j * 128]
        for i in range(3):
            t.append(_pad_tile(np.stack([_km(w_in[l][:, c:c + 128]) for c in chunks[i * 4:(i + 1) * 4]], axis=1)))
        for j in range(8):
            t.append(_pad_tile(np.concatenate([_km(w_in[l][:, 3328 + j * 128:3328 + (j + 1) * 128]),
                                               _km(w_in[l][:, 4352 + j * 128:4352 + (j + 1) * 128]),
                                               _km(proj_a[l][:, j * 128:(j + 1) * 128]),
                                               _km(proj_b[l][:, j * 128:(j + 1) * 128])], axis=1)))
        for i in range(2):
            t.append(_pad_tile(np.stack([_km(w_out[l][:, c * 128:(c + 1) * 128]) for c in range(i * 4, i * 4 + 4)], axis=1)))
        for i in range(11):
            cc = [2 * i * 128, (2 * i + 1) * 128, DFF + 2 * i * 128, DFF + (2 * i + 1) * 128]
            t.append(_pad_tile(np.stack([_km(w_up[l][:, c:c + 128]) for c in cc], axis=1)))
        for j in range(8):
            t.append(_pad_tile(_km(w_down[l][:, j * 128:(j + 1) * 128])))
        assert len(t) == SPL
        wsrc[l * SPL:(l + 1) * SPL] = np.stack(t)
    rows = np.zeros((2, 128, NROW), np.float32)
    cols = np.zeros((128, 2 * NCOL + 8), np.float32)
    lup = np.zeros((2, 128, NLUP), np.float32)
    for l in range(2):
        vecs = [f("k_k")[l], f("k_a")[l], f("r_k")[l].reshape(-1), f("ln_x_w")[l], f("ln_x_b")[l], f("mu_shift")[l]]
        rows[l] = np.broadcast_to(np.concatenate(vecs)[None, :], (128, NROW))
        c = l * NCOL
        cols[:, c:c + 8] = f("norm_mix_g")[l].reshape(8, 128).T
        cols[:, c + 8:c + 16] = f("norm_ffn_g")[l].reshape(8, 128).T
        cols[:, c + 16:c + 28] = f("conv_w")[l].reshape(3, 4, 128).transpose(2, 1, 0).reshape(128, 12)
        cols[:, c + 28:c + 160] = f("ffn_conv_w")[l].reshape(3, 44, 128).transpose(2, 1, 0).reshape(128, 132)
        cols[:, c + 160:c + 204] = f("ffn_conv_b")[l].reshape(44, 128).T
        lup[l, 0:64, 0:512] = f("decay_up")[l]
        lup[l, 64, 0:512] = f("w0")[l]
        lup[l, 0:64, 512:1024] = f("a_up")[l]
        lup[l, 64, 512:1024] = f("a0")[l]
        lup[l, :, 1024:1536] = f("g_up")[l]
        if l == 1:
            lup[l, 0:32, 1536:2048] = f("vres_up")[0]
            lup[l, 32, 1536:2048] = f("v0")[0]
            lup[l, :, 2048:2304] = _km(f("vres_down")[0]).reshape(128, 256)
    cols[:, 2 * NCOL:2 * NCOL + 8] = f("norm_final_g").reshape(8, 128).T
    brows = np.zeros((128, 2, 512), np.float32)
    for r, vec in enumerate([f("w0")[0], f("a0")[0], f("w0")[1], f("a0")[1], f("v0")[0]]):
        brows[32 * (r % 3), r // 3] = vec
    s_, t_ = np.arange(128)[:, None], np.arange(128)[None, :]
    cst = np.concatenate([np.eye(128), (s_ <= t_), (s_ < t_), (s_ > t_)], axis=1).astype(np.float32)
    return dict(wsrc=wsrc, rows=rows, cols=cols, lup=lup, cst=cst, brows=brows.reshape(128, 1024))


_NC_CACHE = {}


N_LAUNCH = 1


def kernel(**inputs):
    x = np.asarray(inputs["x"], np.float32)
    bsz = x.shape[0]
    per = bsz // N_CORES
    pl = per // N_LAUNCH
    shared = host_layout(inputs)
    key = (pl, SEQ // TG)
    if key not in _NC_CACHE:
        _NC_CACHE[key] = build_program(n_seq=pl, n_groups=SEQ // TG)
    nc = _NC_CACHE[key]
    out = np.zeros((bsz, SEQ, D), np.float32)
    for h in range(N_LAUNCH):
        in_maps = []
        for c in range(N_CORES):
            m = dict(shared)
            b0 = c * per + h * pl
            m["x"] = np.ascontiguousarray(x[b0:b0 + pl].reshape(pl * SEQ, D))
            in_maps.append(m)
        res = run_bass_kernel_spmd(nc, in_maps, core_ids=list(range(N_CORES)))
        for c in range(N_CORES):
            b0 = c * per + h * pl
            out[b0:b0 + pl] = np.asarray(res.results[c]["y"], np.float32).reshape(pl, SEQ, D)
    return out
```
